# Optimizing a Trainium2 kernel written in Bass

```python
import jax, jax.numpy as jnp
from jax import lax
import numpy as np

D_MODEL = 1024
BATCH = 8
SEQ = 2048
DEPTH = 1

N_HEADS = 8
HEAD_DIM = 64
ATTN_WIDTH = N_HEADS * HEAD_DIM
IDX_HEADS = 8
IDX_DIM = 64
TOPK_MAX = 256
Q_BLOCK = 128
POOL_WINDOWS = (2, 4, 8, 16)
POOL_GROUPS = len(POOL_WINDOWS)
POOL_GROUP_DIM = 128
POOL_WIDTH = POOL_GROUPS * POOL_GROUP_DIM
D_FF = 2816
ROPE_THETA = 10000.0
EPS = 1e-6

SPLITS = (ATTN_WIDTH, HEAD_DIM, HEAD_DIM, IDX_HEADS * IDX_DIM, IDX_DIM, IDX_HEADS,
          POOL_WIDTH, D_MODEL, D_MODEL)
D_IN = sum(SPLITS)
SPLIT_POINTS = [int(c) for c in np.cumsum(SPLITS)[:-1]]

kernel_name = "hybrid_dsa_pool_macaron_block"


def rms_norm(x, g):
    xf = x.astype(jnp.float32)
    y = xf * lax.rsqrt(jnp.mean(xf * xf, axis=-1, keepdims=True) + EPS)
    return (y * g.astype(jnp.float32)).astype(x.dtype)


def rope_tables(positions, dim):
    inv_freq = ROPE_THETA ** (-jnp.arange(0, dim, 2, dtype=jnp.float32) / dim)
    ang = positions.astype(jnp.float32)[..., None] * inv_freq
    return jnp.cos(ang), jnp.sin(ang)


def apply_rope(x, cos, sin):
    xf = x.astype(jnp.float32)
    half = xf.shape[-1] // 2
    x1, x2 = xf[..., :half], xf[..., half:]
    return jnp.concatenate([x1 * cos - x2 * sin, x2 * cos + x1 * sin], axis=-1).astype(x.dtype)


def swiglu(h, w1, w3, w2):
    return (jax.nn.silu(h @ w1) * (h @ w3)) @ w2


def dsa_attention(q, k, v, qi, ki, wi, topk):
    B, S = q.shape[0], q.shape[1]
    n_blocks = S // Q_BLOCK
    scale = HEAD_DIM ** -0.5
    idx_scale = (IDX_DIM ** -0.5) * (IDX_HEADS ** -0.5)
    key_pos = jnp.arange(S)
    gather = jax.vmap(lambda t, i: t[i])

    def block(bi):
        start = bi * Q_BLOCK
        qb = lax.dynamic_slice_in_dim(q, start, Q_BLOCK, axis=1)
        qib = lax.dynamic_slice_in_dim(qi, start, Q_BLOCK, axis=1)
        wib = lax.dynamic_slice_in_dim(wi, start, Q_BLOCK, axis=1)
        qpos = start + jnp.arange(Q_BLOCK)
        causal = key_pos[None, :] <= qpos[:, None]
        dots = jnp.einsum('bqhd,bsd->bqhs', qib, ki).astype(jnp.float32)
        score = jnp.einsum('bqhs,bqh->bqs', jax.nn.relu(dots), wib.astype(jnp.float32)) * idx_scale
        score = jnp.where(causal[None], score, -jnp.inf)
        _, sel = lax.top_k(score, topk)
        valid = sel <= qpos[None, :, None]
        ks = gather(k, sel)
        vs = gather(v, sel)
        logits = jnp.einsum('bqhd,bqkd->bqhk', qb, ks).astype(jnp.float32) * scale
        logits = jnp.where(valid[:, :, None, :], logits, -jnp.inf)
        p = jax.nn.softmax(logits, axis=-1).astype(v.dtype)
        return jnp.einsum('bqhk,bqkd->bqhd', p, vs)

    out = lax.map(block, jnp.arange(n_blocks))
    return out.transpose(1, 0, 2, 3, 4).reshape(B, S, N_HEADS * HEAD_DIM)


def multiscale_pool(u, w, b, scale):
    B, S = u.shape[0], u.shape[1]
    ug = u.reshape(B, S, POOL_GROUPS, POOL_GROUP_DIM)
    count_base = jnp.arange(1, S + 1, dtype=jnp.float32)[None, :, None]
    outs = []
    for g, win in enumerate(POOL_WINDOWS):
        xg = ug[:, :, g].astype(jnp.float32)
        c = jnp.cumsum(xg, axis=1)
        lag = jnp.pad(c[:, :-win], ((0, 0), (win, 0), (0, 0)))
        mean = (c - lag) / jnp.minimum(count_base, float(win))
        outs.append(mean - xg)
    pooled = jnp.stack(outs, axis=2).astype(u.dtype)
    mixed = jnp.einsum('bsgc,gcd->bsgd', pooled, w) + b
    return mixed.reshape(B, S, POOL_WIDTH) * scale


def setup_inputs(seed: int = 0) -> dict:
    key = jax.random.key(seed)
    ks = jax.random.split(key, 24)
    L = DEPTH
    f32 = jnp.float32

    def nrm(k, shape, fan_in):
        return jax.random.normal(k, shape, f32) * (fan_in ** -0.5)

    def gain(k, shape):
        return 1.0 + 0.02 * jax.random.normal(k, shape, f32)

    return {
        "x": jax.random.normal(ks[0], (BATCH, SEQ, D_MODEL), f32),
        "positions": jnp.broadcast_to(jnp.arange(SEQ, dtype=jnp.int32), (BATCH, SEQ)),
        "ffn1_norm": gain(ks[1], (L, D_MODEL)),
        "ffn1_w1": nrm(ks[2], (L, D_MODEL, D_FF), D_MODEL),
        "ffn1_w3": nrm(ks[3], (L, D_MODEL, D_FF), D_MODEL),
        "ffn1_w2": nrm(ks[4], (L, D_FF, D_MODEL), D_FF),
        "mix_norm": gain(ks[5], (L, D_MODEL)),
        "w_in": nrm(ks[6], (L, D_MODEL, D_IN), D_MODEL),
        "q_norm": gain(ks[7], (L, HEAD_DIM)),
        "k_norm": gain(ks[8], (L, HEAD_DIM)),
        "pool_w": nrm(ks[9], (L, POOL_GROUPS, POOL_GROUP_DIM, POOL_GROUP_DIM), POOL_GROUP_DIM),
        "pool_b": 0.02 * jax.random.normal(ks[10], (L, POOL_GROUPS, POOL_GROUP_DIM), f32),
        "pool_scale": gain(ks[11], (L, POOL_WIDTH)),
        "proj_attn": nrm(ks[12], (L, ATTN_WIDTH, D_MODEL), ATTN_WIDTH),
        "proj_pool": nrm(ks[13], (L, POOL_WIDTH, D_MODEL), POOL_WIDTH),
        "w_out": nrm(ks[14], (L, D_MODEL, D_MODEL), D_MODEL),
        "ffn2_norm": gain(ks[15], (L, D_MODEL)),
        "ffn2_w1": nrm(ks[16], (L, D_MODEL, D_FF), D_MODEL),
        "ffn2_w3": nrm(ks[17], (L, D_MODEL, D_FF), D_MODEL),
        "ffn2_w2": nrm(ks[18], (L, D_FF, D_MODEL), D_FF),
    }


def reference(x, positions, ffn1_norm, ffn1_w1, ffn1_w3, ffn1_w2, mix_norm, w_in, q_norm, k_norm,
              pool_w, pool_b, pool_scale, proj_attn, proj_pool, w_out,
              ffn2_norm, ffn2_w1, ffn2_w3, ffn2_w2):
    B, S = x.shape[0], x.shape[1]
    topk = min(TOPK_MAX, S // 4)
    cos, sin = rope_tables(positions, HEAD_DIM)
    cos_h, sin_h = cos[:, :, None, :], sin[:, :, None, :]

    for l in range(DEPTH):
        x = x + 0.5 * swiglu(rms_norm(x, ffn1_norm[l]), ffn1_w1[l], ffn1_w3[l], ffn1_w2[l])

        h = rms_norm(x, mix_norm[l])
        z = h @ w_in[l]
        q, k, v, qi, ki, wi, u, g_attn, g_pool = jnp.split(z, SPLIT_POINTS, axis=-1)

        q = apply_rope(rms_norm(q.reshape(B, S, N_HEADS, HEAD_DIM), q_norm[l]), cos_h, sin_h)
        k = apply_rope(rms_norm(k, k_norm[l]), cos, sin)
        qi = apply_rope(qi.reshape(B, S, IDX_HEADS, IDX_DIM), cos_h, sin_h)
        ki = apply_rope(ki, cos, sin)
        y_attn = dsa_attention(q, k, v, qi, ki, wi, topk)

        y_pool = multiscale_pool(u, pool_w[l], pool_b[l], pool_scale[l])

        merged = (jax.nn.sigmoid(g_attn) * (y_attn @ proj_attn[l])
                  + jax.nn.sigmoid(g_pool) * (y_pool @ proj_pool[l]))
        x = x + merged @ w_out[l]

        x = x + 0.5 * swiglu(rms_norm(x, ffn2_norm[l]), ffn2_w1[l], ffn2_w3[l], ffn2_w2[l])
    return x
```

```python
import contextlib
import numpy as np
import concourse.bass as bass
import concourse.mybir as mybir
from concourse.bass_utils import run_bass_kernel_spmd

F32 = mybir.dt.float32
BF16 = mybir.dt.bfloat16
I32 = mybir.dt.int32
U8 = mybir.dt.uint8
ALU = mybir.AluOpType
AF = mybir.ActivationFunctionType
AX = mybir.AxisListType

D = 1024
S_LEN = 2048
NT = 16
DFF = 2816
NFC = 22
DIN = 3784
EPS = 1e-6
NIT = 14
TOPK = 256
NEG = -1.0e30
MASK_BIG = 30000.0
C_Q, C_K, C_V, C_QI, C_KI, C_WI, C_U, C_GA, C_GP = 0, 512, 576, 640, 1152, 1216, 1224, 1736, 2760


class Buf:
    __slots__ = ("name", "w", "r")

    def __init__(self, name=""):
        self.name = name
        self.w = None
        self.r = []


class Op:
    __slots__ = ("q", "fn", "deps", "dma", "key", "semval", "signal", "count", "id", "t_end")


class Sched:
    QUEUES = ("pe", "act", "dve", "pool", "sp")

    def __init__(self, nc):
        self.nc = nc
        self.ops = []
        self.dma_count = {}
        self.eng_free = {q: 0.0 for q in self.QUEUES}
        self.step_end = 0.0

    DEF_DUR = {"pe": 0.25, "act": 0.75, "dve": 0.45, "pool": 1.4, "sp": 0.1}
    LAT = 0.25

    def add(self, q, fn, reads=(), writes=(), dma=False, key=None, dur=None):
        op = Op()
        op.q, op.fn, op.dma, op.key = q, fn, dma, key
        op.id = len(self.ops)
        op.signal = False
        op.count = 0
        op.semval = 0
        deps = {}
        for b in reads:
            if b.w is not None:
                deps[b.w] = True
        for b in writes:
            if b.w is not None:
                deps.setdefault(b.w, False)
            for r in b.r:
                deps.setdefault(r, False)
        deps.pop(op.id, None)
        op.deps = deps
        if dma:
            if key is None:
                key = op.key = "dma%d" % op.id
            self.dma_count[key] = self.dma_count.get(key, 0) + 1
            op.semval = 16 * self.dma_count[key]
        t0 = self.eng_free[q]
        for did in deps:
            d = self.ops[did]
            if self._need(op, d, True):
                t0 = max(t0, d.t_end + self.LAT)
        if dma:
            op.t_end = t0 + (dur if dur is not None else 6.0)
            self.eng_free[q] = t0 + 0.15
        else:
            op.t_end = t0 + (dur if dur is not None else self.DEF_DUR[q])
            self.eng_free[q] = op.t_end
        self.step_end = max(self.step_end, op.t_end)
        for b in reads:
            b.r.append(op.id)
        for b in writes:
            b.w = op.id
            b.r = []
        self.ops.append(op)
        return op.id

    def _need(self, op, d, raw):
        if d.dma:
            return True
        if d.q == op.q and not op.dma:
            return op.q != "pe"
        return True

    def emit(self, final_wait=()):
        nc = self.nc
        ops = self.ops
        for op in ops:
            for did, raw in op.deps.items():
                if self._need(op, ops[did], raw):
                    ops[did].signal = True
        for fid in final_wait:
            ops[fid].signal = True
        cnt = {q: 0 for q in self.QUEUES}
        for op in ops:
            if not op.dma and op.signal:
                cnt[op.q] += 1
                op.count = cnt[op.q]
        with contextlib.ExitStack() as es:
            qsem = {q: es.enter_context(nc.semaphore("s_" + q)) for q in ("pe", "act", "dve", "pool")}
            dsem = {k: es.enter_context(nc.semaphore("d_" + str(k))) for k in self.dma_count}
            block = es.enter_context(nc.Block())
            byq = {q: [o for o in ops if o.q == q] for q in self.QUEUES}

            def run(q, eng):
                waited = {}

                def do_waits(need):
                    for nm, (sem, val) in need.items():
                        if waited.get(nm, 0) >= val:
                            continue
                        waited[nm] = val
                        eng.wait_ge(sem, val)

                for op in byq[q]:
                    need = {}
                    for did, raw in op.deps.items():
                        d = ops[did]
                        if not self._need(op, d, raw):
                            continue
                        if d.dma:
                            nm, sem, val = "d_" + str(d.key), dsem[d.key], d.semval
                        else:
                            nm, sem, val = "q_" + d.q, qsem[d.q], d.count
                        if nm not in need or need[nm][1] < val:
                            need[nm] = (sem, val)
                    do_waits(need)
                    ins = op.fn(eng)
                    if op.dma:
                        ins.then_inc(dsem[op.key], 16)
                    elif op.signal:
                        ins.then_inc(qsem[q], 1)
                if q == "sp":
                    need = {}
                    for fid in final_wait:
                        d = ops[fid]
                        nm = "d_" + str(d.key)
                        if nm not in need or need[nm][1] < d.semval:
                            need[nm] = (dsem[d.key], d.semval)
                    do_waits(need)

            @block.tensor
            def _(e):
                run("pe", e)

            @block.scalar
            def _(e):
                run("act", e)

            @block.vector
            def _(e):
                run("dve", e)

            @block.gpsimd
            def _(e):
                run("pool", e)

            @block.sync
            def _(e):
                run("sp", e)


def MM(out, lhsT, rhs, start, stop):
    return lambda e: e.matmul(out, lhsT=lhsT, rhs=rhs, start=start, stop=stop)


def TR(out, in_, ident):
    return lambda e: e.transpose(out=out, in_=in_, identity=ident)


def ACT(out, in_, func, **kw):
    return lambda e: e.activation(out=out, in_=in_, func=func, **kw)


def TT(out, in0, in1, op):
    return lambda e: e.tensor_tensor(out=out, in0=in0, in1=in1, op=op)


def TS(out, in0, s1, s2, op0, op1=None, accum=None):
    if accum is not None:
        return lambda e: e.tensor_scalar(out=out, in0=in0, scalar1=s1, scalar2=s2, op0=op0, op1=op1, accum_out=accum)
    if op1 is None:
        return lambda e: e.tensor_scalar(out=out, in0=in0, scalar1=s1, scalar2=None, op0=op0)
    return lambda e: e.tensor_scalar(out=out, in0=in0, scalar1=s1, scalar2=s2, op0=op0, op1=op1)


def STT(out, in0, scalar, in1, op0, op1):
    return lambda e: e.scalar_tensor_tensor(out=out, in0=in0, scalar=scalar, in1=in1, op0=op0, op1=op1)


def CP(out, in_):
    return lambda e: e.tensor_copy(out=out, in_=in_)


def DMA(out, in_):
    return lambda e: e.dma_start(out=out, in_=in_)


def build_nc(debug=None):
    debug = debug or {}
    nc = bass.Bass("TRN2", target_bir_lowering=False)
    dr = {}

    def din(name, shape, dt=F32):
        dr[name] = nc.dram_tensor(name, list(shape), dt, kind="ExternalInput").ap()
        return dr[name]

    x_d = din("x", [S_LEN, D])
    pos_d = din("pos", [128, NT], I32)
    cst_d = din("cst", [128, 256])
    w1_d = [din("ffn1_w1", [D, DFF]), din("ffn2_w1", [D, DFF])]
    w3_d = [din("ffn1_w3", [D, DFF]), din("ffn2_w3", [D, DFF])]
    w2_d = [din("ffn1_w2", [DFF, D]), din("ffn2_w2", [DFF, D])]
    win_d = din("w_in", [D, DIN])
    poolw_d = din("pool_w", [4, 128, 128])
    pa_d = din("proj_attn", [512, D])
    pp_d = din("proj_pool", [512, D])
    wout_d = din("w_out", [D, D])
    out_d = nc.dram_tensor("out", [S_LEN, D], F32, kind="ExternalOutput").ap()
    dbg_d = {k: nc.dram_tensor("dbg_" + k, list(shp), F32, kind="ExternalOutput").ap() for k, shp in debug.items()}

    S = Sched(nc)
    ARENA = 212000
    with contextlib.ExitStack() as es:
        arena = es.enter_context(nc.sbuf_tensor("arena", [128, ARENA], U8))
        pp = es.enter_context(nc.psum_tensor("pp", [128, 4096], F32))

        def view(off, nbytes, dt):
            return arena[:, off:off + nbytes].bitcast(dt)

        PB = [Buf("bank%d" % b) for b in range(8)]

        def bank(b, n=512):
            return pp[:, b * 512:b * 512 + n]

        def bank_bf(b):
            return pp[:, b * 512:(b + 1) * 512].bitcast(BF16)

        xs = view(0, 65536, F32).rearrange("p (t d) -> p t d", t=NT)
        XB = [Buf("x%d" % t) for t in range(NT)]
        slot_ap = [view(65536 + 8192 * k, 8192, BF16) for k in range(4)]
        SLB = [Buf("slot%d" % k) for k in range(4)]
        CB = 98304
        ident = view(CB, 256, BF16)
        identf = view(CB + 256, 512, F32)
        cmask = view(CB + 768, 512, F32)
        cst = view(CB + 1280, 1024, F32)
        cosT = view(CB + 2304, 2048, F32).rearrange("p (t i) -> p t i", t=NT)
        sinT = view(CB + 4352, 2048, F32).rearrange("p (t i) -> p t i", t=NT)
        nsinT = view(CB + 6400, 2048, F32).rearrange("p (t i) -> p t i", t=NT)
        posi = view(CB + 8448, 64, I32)
        posf = view(CB + 8512, 64, F32)
        ssb = view(CB + 8576, 64, F32)
        rstd = view(CB + 8640, 64, F32)
        wabs = view(CB + 8704, 512, F32).rearrange("p (t h) -> p t h", t=NT)
        wsgn = view(CB + 9216, 512, F32).rearrange("p (t h) -> p t h", t=NT)
        poolw = view(CB + 9728, 1024, BF16).rearrange("p (g d) -> p g d", g=4)
        bsc = view(CB + 10752, 16, F32)
        sm = view(CB + 10768, 2048, F32)
        ones_bf = view(CB + 12816, 128, BF16)
        negbig = view(CB + 12944, 4, F32)
        CST = Buf("cst")
        IDB = Buf("ident")
        TAB = Buf("tables")
        SSB = Buf("ss")
        RSB = Buf("rstd")
        WAB = Buf("wabs")
        SMB = Buf("sm")
        PWB = Buf("poolw")
        gcol = cst[:, 0:24].rearrange("p (k c) -> p k c", k=3)
        poolb = cst[:, 24:28]
        pools = cst[:, 28:32]
        gq = cst[:, 32:96]
        gk = cst[:, 96:160]
        invf = cst[:, 160:192]
        icnt = cst[:, 192:208]
        mhalf = cst[:, 208:224]
        pow2 = cst[:, 224:256]

        PH = 111616

        slot_i = [0]

        def load_w(viewfn, dram_ap):
            k = slot_i[0] % 4
            slot_i[0] += 1
            v = viewfn(slot_ap[k])
            S.add("pool", DMA(v, dram_ap), writes=[SLB[k]], dma=True, key="slot%d" % k)
            return v, SLB[k]

        def dbg(name, ap, bufs):
            if name in dbg_d:
                q = "sp" if ap.dtype == F32 else "pool"
                S.add(q, DMA(dbg_d[name], ap), reads=bufs, dma=True)

        S.add("sp", DMA(cst, cst_d), writes=[CST], dma=True)
        S.add("sp", DMA(posi, pos_d), writes=[TAB], dma=True)
        for t in range(NT):
            S.add("sp", DMA(xs[:, t, :], x_d[t * 128:(t + 1) * 128, :]), writes=[XB[t]], dma=True, key="x%d" % t)
        S.add("pool", lambda e: e.memset(identf, 1.0), writes=[IDB])
        S.add("pool", lambda e: e.affine_select(out=identf, in_=identf, pattern=[[-1, 128]], compare_op=ALU.is_equal,
                                                fill=0.0, base=0, channel_multiplier=1), reads=[IDB], writes=[IDB])
        S.add("dve", CP(ident, identf), reads=[IDB], writes=[IDB])
        S.add("pool", lambda e: e.memset(cmask, 0.0), writes=[IDB])
        S.add("pool", lambda e: e.affine_select(out=cmask, in_=cmask, pattern=[[-1, 128]], compare_op=ALU.is_ge,
                                                fill=NEG, base=0, channel_multiplier=1), reads=[IDB], writes=[IDB])
        S.add("pool", lambda e: e.memset(ones_bf, 1.0), writes=[IDB])
        S.add("pool", lambda e: e.memset(negbig, -MASK_BIG), writes=[IDB])
        S.add("pool", DMA(poolw, poolw_d.rearrange("g c d -> c g d")), writes=[PWB], dma=True)
        def setup_tables():
            S.add("dve", CP(posf, posi), reads=[TAB], writes=[TAB])
            ang = sm.rearrange("p (t i) -> p t i", t=NT)
            S.add("dve", TT(ang, posf.unsqueeze(2).to_broadcast([128, NT, 32]), invf.unsqueeze(1).to_broadcast([128, NT, 32]), ALU.mult),
                  reads=[TAB, CST], writes=[SMB])
            PI = float(np.pi)
            TWO_PI = 2.0 * PI
            scr_i = view(PH + 86016, 2048, I32).rearrange("p (t i) -> p t i", t=NT)
            scr_f = view(PH + 88064, 2048, F32).rearrange("p (t i) -> p t i", t=NT)
            scr_c = view(PH + 90112, 2048, F32).rearrange("p (t i) -> p t i", t=NT)
            SCR = Buf("scr")

            def reduce_angle(dst, shift):
                S.add("dve", TS(dst, ang, shift, None, ALU.add), reads=[SMB], writes=[TAB])
                S.add("dve", TS(scr_f, dst, 1.0 / TWO_PI, None, ALU.mult), reads=[TAB], writes=[SCR])
                S.add("dve", CP(scr_i, scr_f), reads=[SCR], writes=[SCR])
                S.add("dve", CP(scr_f, scr_i), reads=[SCR], writes=[SCR])
                S.add("dve", STT(dst, scr_f, -TWO_PI, dst, ALU.mult, ALU.add), reads=[SCR, TAB], writes=[TAB])
                S.add("dve", TS(scr_c, dst, PI, -TWO_PI, ALU.is_gt, ALU.mult), reads=[TAB], writes=[SCR])
                S.add("dve", TT(dst, dst, scr_c, ALU.add), reads=[TAB, SCR], writes=[TAB])
                S.add("dve", TS(scr_c, dst, -PI, TWO_PI, ALU.is_lt, ALU.mult), reads=[TAB], writes=[SCR])
                S.add("dve", TT(dst, dst, scr_c, ALU.add), reads=[TAB, SCR], writes=[TAB])
                S.add("dve", TS(dst, dst, 3.141592, -3.141592, ALU.min, ALU.max), reads=[TAB], writes=[TAB])
            reduce_angle(sinT, 0.0)
            reduce_angle(cosT, 0.5 * PI)
            S.add("act", ACT(sinT, sinT, AF.Sin), reads=[TAB], writes=[TAB])
            S.add("act", ACT(cosT, cosT, AF.Sin), reads=[TAB], writes=[TAB])
            S.add("dve", TS(nsinT, sinT, -1.0, None, ALU.mult), reads=[TAB], writes=[TAB])

        S.add("dve", TT(bsc, poolb, pools, ALU.mult), reads=[CST], writes=[PWB])

        SSQ = [Buf("ssq%d" % i) for i in range(4)]
        RSQ = [Buf("rsq%d" % i) for i in range(4)]

        def norm_stats(junks, JBs, quarters=(0, 1, 2, 3)):
            for qq in quarters:
                for t in range(qq * 4, qq * 4 + 4):
                    S.add("act", ACT(junks[t % 2], xs[:, t, :], AF.Square, accum_out=ssb[:, t:t + 1]), reads=[XB[t]], writes=[JBs[t % 2], SSQ[qq]])
                sl_ = slice(qq * 4, qq * 4 + 4)
                S.add("dve", TS(ssb[:, sl_], ssb[:, sl_], 1.0 / D, EPS, ALU.mult, ALU.add), reads=[SSQ[qq]], writes=[SSQ[qq]])
                S.add("pool", TT(rstd[:, sl_], ssb[:, sl_], mhalf[:, 0:4], ALU.pow), reads=[SSQ[qq], CST], writes=[RSQ[qq]])

        def norm_tile(t, hn, HNB, tpb, gidx, dstT, dst_bufs):
            S.add("act", ACT(hn, xs[:, t, :], AF.Copy, scale=rstd[:, t:t + 1]), reads=[XB[t], RSQ[t // 4]], writes=[HNB])
            tp = bank_bf(tpb)
            for c in range(8):
                S.add("pe", TR(tp[:, c * 128:(c + 1) * 128], hn[:, c * 128:(c + 1) * 128], ident), reads=[HNB, IDB], writes=[PB[tpb]])
            S.add("dve", TT(dstT, tp.rearrange("p (c t) -> p c t", c=8), gcol[:, gidx, :].unsqueeze(2).to_broadcast([128, 8, 128]), ALU.mult),
                  reads=[PB[tpb], CST], writes=dst_bufs)

        def ffn(k, final):
            hT = view(PH, 32768, BF16).rearrange("p (c t) -> p c t", c=8)
            gT = view(PH + 32768, 45056, BF16).rearrange("p (f t) -> p f t", f=11)
            sl = [view(PH + 77824 + 1024 * i, 1024, BF16) for i in range(2)]
            junk = view(PH + 79872, 2048, BF16)
            hn = [view(PH + 81920 + 2048 * i, 2048, BF16) for i in range(2)]
            HTB = [Buf("hT%d" % t) for t in range(NT)]
            GTB = [Buf("gT%d" % q) for q in range(4)]
            SLBf = [Buf("sl0"), Buf("sl1")]
            JB = Buf("junk")
            HNB = [Buf("hn0"), Buf("hn1")]
            gidx = 0 if k == 0 else 2
            stores = []
            step = 0

            def phase_a_step(w1v, w1b, w3v, w3b, fl, fcl, q):
                nonlocal step
                b1 = step % 2
                b3 = 2 + step % 2
                step += 1
                hb = [HTB[q * 4 + i] for i in range(4)]
                for c in range(8):
                    S.add("pe", MM(bank(b1), w1v[:, c, fl * 128:(fl + 1) * 128], hT[:, c, q * 512:(q + 1) * 512], c == 0, c == 7),
                          reads=[w1b] + hb, writes=[PB[b1]])
                for c in range(8):
                    S.add("pe", MM(bank(b3), w3v[:, c, fl * 128:(fl + 1) * 128], hT[:, c, q * 512:(q + 1) * 512], c == 0, c == 7),
                          reads=[w3b] + hb, writes=[PB[b3]])
                S.add("act", ACT(sl[b1], bank(b1), AF.Silu), reads=[PB[b1]], writes=[SLBf[b1]])
                S.add("dve", TT(gT[:, fcl, q * 512:(q + 1) * 512], sl[b1], bank(b3), ALU.mult),
                      reads=[SLBf[b1], PB[b3]], writes=[GTB[q]])

            for fh in range(2):
                groups = [(0, 4), (4, 4), (8, 3)]
                for gi, (f0, nf) in enumerate(groups):
                    col0 = (fh * 11 + f0) * 128
                    vf = lambda s, nf=nf: s[:, 0:8 * nf * 128].rearrange("p (c f) -> p c f", c=8)
                    w1v, w1b = load_w(vf, w1_d[k][:, col0:col0 + nf * 128].rearrange("(c p) f -> p c f", p=128))
                    w3v, w3b = load_w(vf, w3_d[k][:, col0:col0 + nf * 128].rearrange("(c p) f -> p c f", p=128))
                    if fh == 0 and gi == 0:
                        def nq(q):
                            norm_stats(hn, HNB, quarters=(q,))
                            for t in range(q * 4, q * 4 + 4):
                                norm_tile(t, hn[t % 2], HNB[t % 2], 4 + (t % 2), gidx, hT[:, :, t * 128:(t + 1) * 128], [HTB[t]])
                        nq(0)
                        for q in range(4):
                            if q < 3:
                                norm_stats(hn, HNB, quarters=(q + 1,))
                            for fl in range(nf):
                                phase_a_step(w1v, w1b, w3v, w3b, fl, f0 + fl, q)
                                if q < 3:
                                    t = (q + 1) * 4 + fl
                                    norm_tile(t, hn[t % 2], HNB[t % 2], 4 + (t % 2), gidx, hT[:, :, t * 128:(t + 1) * 128], [HTB[t]])
                        if k == 0:
                            setup_tables()
                    else:
                        for fl in range(nf):
                            for q in range(4):
                                phase_a_step(w1v, w1b, w3v, w3b, fl, f0 + fl, q)
                for tg in range(2):
                    for dh in range(2):
                        for (f0, nf) in groups:
                            row0 = (fh * 11 + f0) * 128
                            vf2 = lambda s, nf=nf: s[:, 0:nf * 512].rearrange("p (c d) -> p c d", c=nf)
                            w2v, w2b = load_w(vf2, w2_d[k][row0:row0 + nf * 128, dh * 512:(dh + 1) * 512].rearrange("(c p) d -> p c d", p=128))
                            for fl in range(nf):
                                fcl = f0 + fl
                                for ti in range(8):
                                    t = tg * 8 + ti
                                    S.add("pe", MM(bank(ti), gT[:, fcl, t * 128:(t + 1) * 128], w2v[:, fl, :], fcl == 0, fcl == 10),
                                          reads=[w2b, GTB[t // 4]], writes=[PB[ti]])
                        for ti in range(8):
                            t = tg * 8 + ti
                            xv = xs[:, t, dh * 512:(dh + 1) * 512]
                            S.add("dve", STT(xv, bank(ti), 0.5, xv, ALU.mult, ALU.add), reads=[PB[ti], XB[t]], writes=[XB[t]])
                            if final and fh == 1 and dh == 1:
                                stores.append(S.add("sp", DMA(out_d[t * 128:(t + 1) * 128, :], xs[:, t, :]), reads=[XB[t]], dma=True, key="st%d" % t))
            return stores

        def mix():
            P0 = PH
            kT2 = view(P0, 4096, BF16)
            kiT2 = view(P0 + 4096, 4096, BF16)
            vaug = view(P0 + 8192, 4096, BF16).rearrange("p (t d) -> p t d", t=NT)
            ypT = view(P0 + 12288, 16384, BF16).rearrange("p (g t) -> p g t", g=4)
            hTc = view(P0 + 28672, 8192, BF16).rearrange("p (c t) -> p c t", c=8)
            Q0 = P0 + 36864
            KTB = [Buf("kT%d" % c) for c in range(4)]
            VB = [Buf("v%d" % c) for c in range(4)]
            YPB = [Buf("yp%d" % c) for c in range(4)]
            HCB = Buf("hTc")


            ub = [view(Q0 + 8448 * i, 8448, F32).rearrange("p (g t) -> p g t", g=4) for i in range(2)]
            sA = view(Q0 + 16896, 8448, F32).rearrange("p (g t) -> p g t", g=4)
            sB = view(Q0 + 25344, 8448, F32).rearrange("p (g t) -> p g t", g=4)
            pooled = view(Q0 + 33792, 4096, BF16).rearrange("p (g t) -> p g t", g=4)
            zk = view(Q0 + 37888, 3200, F32).rearrange("p (t c) -> p t c", t=4)
            kt = [view(Q0 + 41088 + 1024 * i, 1024, F32).rearrange("p (t c) -> p t c", t=4) for i in range(4)]
            kk = [view(Q0 + 45184 + 1024 * i, 1024, BF16).rearrange("p (t c) -> p t c", t=4) for i in range(2)]
            hn = [view(Q0 + 47232 + 2048 * i, 2048, BF16) for i in range(2)]
            UBB = [Buf("ub0"), Buf("ub1")]
            SAB, SBB, PLB, ZKB, KKB = Buf("sA"), Buf("sB"), Buf("pooled"), Buf("zk"), Buf("kk")
            KTMP = Buf("ktmp")
            HNB = [Buf("hn0"), Buf("hn1")]
            norm_stats(hn, HNB)

            def vkv(s):
                return s[:, 0:8 * 200].rearrange("p (c f) -> p c f", c=8)
            kslot = slot_i[0] % 4
            slot_i[0] += 1
            wkv = vkv(slot_ap[kslot])
            wkvb = SLB[kslot]
            for (dst0, src0, n) in [(0, C_K, 64), (64, C_KI, 64), (128, C_V, 64), (192, C_WI, 8)]:
                S.add("pool", DMA(wkv[:, :, dst0:dst0 + n], win_d[:, src0:src0 + n].rearrange("(c p) f -> p c f", p=128)),
                      writes=[wkvb], dma=True, key="kv%d" % dst0)
            wu, wub = load_w(lambda s: s.rearrange("p (c f) -> p c f", c=8), win_d[:, C_U:C_U + 512].rearrange("(c p) f -> p c f", p=128))

            zk2 = [zk, view(Q0 + 51328, 3200, F32).rearrange("p (t c) -> p t c", t=4)]
            kt2 = [kt, [view(Q0 + 54528 + 1024 * i, 1024, F32).rearrange("p (t c) -> p t c", t=4) for i in range(4)]]
            kk2 = [kk, [view(Q0 + 58624 + 1024 * i, 1024, BF16).rearrange("p (t c) -> p t c", t=4) for i in range(2)]]
            ZKB2, KTMP2, KKB2, SMK = [ZKB, Buf("zk1")], [KTMP, Buf("ktmp1")], [KKB, Buf("kk1")], [Buf("smk0"), Buf("smk1")]

            def p1_thread(ch):
                p = ch % 2
                zk_, kt_, kk_ = zk2[p], kt2[p], kk2[p]
                ZKB_, KTMP_, KKB_, SMK_ = ZKB2[p], KTMP2[p], KKB2[p], SMK[p]
                cur, prv = ub[ch % 2], ub[(ch + 1) % 2]
                CUB, PVB = UBB[ch % 2], UBB[(ch + 1) % 2]
                yield ("acquire", "p1set%d" % p)
                yield ("acquire", "hTc")
                for tl in range(4):
                    t = ch * 4 + tl
                    norm_tile(t, hn[t % 2], HNB[t % 2], 4 + (t % 2), 1, hTc[:, :, tl * 128:(tl + 1) * 128], [HCB])
                    yield None
                for tl in range(4):
                    b = tl % 2
                    for c in range(8):
                        S.add("pe", MM(bank(b, 200), hTc[:, c, tl * 128:(tl + 1) * 128], wkv[:, c, :], c == 0, c == 7),
                              reads=[HCB, wkvb], writes=[PB[b]], dur=0.12)
                    S.add("act", ACT(zk_[:, tl, :], bank(b, 200), AF.Copy), reads=[PB[b]], writes=[ZKB_], dur=0.45)
                    yield None
                for g in range(4):
                    b = 2 + (g % 2)
                    for c in range(8):
                        S.add("pe", MM(bank(b), wu[:, c, g * 128:(g + 1) * 128], hTc[:, c, :], c == 0, c == 7),
                              reads=[HCB, wub], writes=[PB[b]], dur=0.25)
                    S.add("act", ACT(cur[:, g, 16:528], bank(b), AF.Copy), reads=[PB[b]], writes=[CUB], dur=0.75)
                    yield None
                yield ("release", "hTc")
                if ch > 0:
                    S.add("dve", CP(cur[:, :, 0:16], prv[:, :, 512:528]), reads=[PVB], writes=[CUB], dur=0.2)
                tsl = slice(ch * 4, ch * 4 + 4)
                kraw = zk_[:, :, 0:64]
                S.add("dve", TT(kt_[0], kraw, kraw, ALU.mult), reads=[ZKB_], writes=[KTMP_], dur=0.4)
                ssk = sm[:, 8 * p:8 * p + 4]
                S.add("dve", lambda e, o=ssk, i=kt_[0]: e.tensor_reduce(out=o, in_=i, axis=AX.X, op=ALU.add), reads=[KTMP_], writes=[SMK_], dur=0.4)
                S.add("dve", TS(ssk, ssk, 1.0 / 64, EPS, ALU.mult, ALU.add), reads=[SMK_], writes=[SMK_], dur=0.2)
                rk = sm[:, 8 * p + 4:8 * p + 8]
                S.add("pool", TT(rk, ssk, mhalf[:, 0:4], ALU.pow), reads=[SMK_, CST], writes=[SMK_], dur=1.4)
                yield None
                S.add("dve", TT(kt_[0], kraw, rk.unsqueeze(2).to_broadcast([128, 4, 64]), ALU.mult), reads=[ZKB_, SMK_], writes=[KTMP_], dur=0.4)
                S.add("dve", TT(kt_[0], kt_[0], gk.unsqueeze(1).to_broadcast([128, 4, 64]), ALU.mult), reads=[KTMP_, CST], writes=[KTMP_], dur=0.4)
                yield None

                def rope4(src, dst_list, RB_, WBs):
                    s4 = src.rearrange("p t (h i) -> p t h i", h=2)
                    cb = cosT[:, tsl, :].unsqueeze(2).to_broadcast([128, 4, 2, 32])
                    S.add("dve", TT(kt_[1].rearrange("p t (h i) -> p t h i", h=2), s4, cb, ALU.mult), reads=[RB_, TAB], writes=[KTMP_], dur=0.4)
                    S.add("dve", TT(kt_[2][:, :, 0:32], src[:, :, 32:64], nsinT[:, tsl, :], ALU.mult), reads=[RB_, TAB], writes=[KTMP_], dur=0.3)
                    S.add("dve", TT(kt_[2][:, :, 32:64], src[:, :, 0:32], sinT[:, tsl, :], ALU.mult), reads=[RB_, TAB], writes=[KTMP_], dur=0.3)
                    for d in dst_list:
                        S.add("dve", TT(d, kt_[1], kt_[2], ALU.add), reads=[KTMP_], writes=WBs, dur=0.4)
                rope4(kt_[0], [kk_[0][:, :, 0:64], kk_[0][:, :, 64:128]], KTMP_, [KKB_])
                yield None
                S.add("dve", CP(kt_[3], zk_[:, :, 64:128]), reads=[ZKB_], writes=[KTMP_], dur=0.3)
                rope4(kt_[3], [kk_[1][:, :, 0:64], kk_[1][:, :, 64:128]], KTMP_, [KKB_])
                yield None
                S.add("dve", CP(vaug[:, tsl, 0:64], zk_[:, :, 128:192]), reads=[ZKB_], writes=[VB[ch]], dur=0.3)
                S.add("dve", lambda e, o=vaug[:, tsl, 64:128]: e.memset(o, 1.0), writes=[VB[ch]], dur=0.2)
                S.add("dve", STT(wabs[:, tsl, :], zk_[:, :, 192:200], -1.0, zk_[:, :, 192:200], ALU.mult, ALU.max), reads=[ZKB_], writes=[WAB], dur=0.2)
                S.add("dve", TS(wsgn[:, tsl, :], zk_[:, :, 192:200], 0.0, 2.0, ALU.is_ge, ALU.mult), reads=[ZKB_], writes=[WAB], dur=0.2)
                S.add("dve", TS(wsgn[:, tsl, :], wsgn[:, tsl, :], -1.0, None, ALU.add), reads=[WAB], writes=[WAB], dur=0.2)
                yield None
                tpk = bank_bf(6)
                for tl in range(4):
                    S.add("pe", TR(tpk[:, tl * 128:(tl + 1) * 128], kk_[0][:, tl, :], ident), reads=[KKB_, IDB], writes=[PB[6]], dur=0.3)
                    S.add("pe", TR(tpk[:, 512 + tl * 128:512 + (tl + 1) * 128], kk_[1][:, tl, :], ident), reads=[KKB_, IDB], writes=[PB[6]], dur=0.3)
                S.add("act", ACT(kT2[:, ch * 512:(ch + 1) * 512], tpk[:, 0:512], AF.Copy), reads=[PB[6]], writes=[KTB[ch]], dur=0.75)
                S.add("act", ACT(kiT2[:, ch * 512:(ch + 1) * 512], tpk[:, 512:1024], AF.Copy), reads=[PB[6]], writes=[KTB[ch]], dur=0.75)
                yield None
                yield ("acquire", "pool")
                u = cur
                S.add("dve", TT(sA[:, :, 1:528], u[:, :, 1:528], u[:, :, 0:527], ALU.add), reads=[CUB], writes=[SAB], dur=2.3)
                yield None
                S.add("dve", TT(sB[:, 1:4, 3:528], sA[:, 1:4, 3:528], sA[:, 1:4, 1:526], ALU.add), reads=[SAB], writes=[SBB], dur=1.8)
                yield None
                S.add("dve", TT(sA[:, 2:4, 7:528], sB[:, 2:4, 7:528], sB[:, 2:4, 3:524], ALU.add), reads=[SBB], writes=[SAB], dur=1.2)
                S.add("dve", TT(sB[:, 3:4, 15:528], sA[:, 3:4, 15:528], sA[:, 3:4, 7:520], ALU.add), reads=[SAB], writes=[SBB], dur=0.7)
                yield None
                srcs = [(sA, SAB, 0, 2), (sB, SBB, 1, 4), (sA, SAB, 2, 8), (sB, SBB, 3, 16)]
                for (sv, svb, g, w) in srcs:
                    S.add("dve", STT(pooled[:, g, :], sv[:, g, 16:528], 1.0 / w, u[:, g, 16:528], ALU.mult, ALU.subtract),
                          reads=[svb, CUB], writes=[PLB], dur=0.7)
                    if ch == 0:
                        n = w - 1
                        S.add("dve", TT(sm[:, 16:16 + n], sv[:, g, 16:16 + n], icnt[:, 0:n], ALU.mult), reads=[svb, CST], writes=[SMB], dur=0.2)
                        S.add("dve", TT(pooled[:, g, 0:n], sm[:, 16:16 + n], u[:, g, 16:16 + n], ALU.subtract), reads=[SMB, CUB], writes=[PLB], dur=0.2)
                    yield None
                for g in range(4):
                    b = g % 2
                    S.add("pe", MM(bank(b), poolw[:, g, :], pooled[:, g, :], True, True), reads=[PWB, PLB], writes=[PB[b]], dur=0.25)
                    S.add("act", ACT(ypT[:, g, ch * 512:(ch + 1) * 512], bank(b), AF.Identity, bias=bsc[:, g:g + 1], scale=pools[:, g:g + 1]),
                          reads=[PB[b], PWB, CST], writes=[YPB[ch]], dur=0.75)
                yield ("release", "pool")

            def pass1():
                S.add("dve", lambda e: e.memset(ub[0][:, :, 0:16], 0.0), writes=[UBB[0]])
                run_threads([p1_thread(ch) for ch in range(4)], max_active=2)
            dbg("kT2", kT2, KTB)
            dbg("kiT2", kiT2, KTB)
            dbg("ypT", ypT, YPB)
            dbg("vaug", vaug, VB)
            dbg("wabs", wabs, [WAB])
            dbg("wsgn", wsgn, [WAB])

            qT = view(Q0, 4096, BF16).rearrange("p (j t) -> p j t", j=4)
            qiT = view(Q0 + 4096, 4096, BF16).rearrange("p (j t) -> p j t", j=4)
            yaT = view(Q0 + 8192, 4096, BF16).rearrange("p (j t) -> p j t", j=4)
            NTH = 3
            sc = [view(Q0 + 12288 + 8192 * i, 8192, F32) for i in range(NTH)]
            Rr = [view(Q0 + 36864 + 2048 * i, 2048, F32) for i in range(2)]
            maskb = view(Q0 + 40960, 4096, BF16)
            Eb = [view(Q0 + 45056 + 2048 * i, 2048, BF16) for i in range(2)]
            rcp = view(Q0 + 45056, 4096, F32)
            mTs = [view(Q0 + 49152 + 4096 * i, 4096, BF16) for i in range(NTH)]
            hn2 = [view(Q0 + 49152 + 2048 * i, 2048, BF16) for i in range(2)]
            qr4 = [view(Q0 + 53248 + 1024 * i, 1024, BF16) for i in range(4)]
            QR4 = [Buf("qr%d" % i) for i in range(4)]
            QTB, QITB, YAB = Buf("qT"), Buf("qiT"), Buf("yaT")
            SCB = [[Buf("sc%d_%d" % (i, c)) for c in range(4)] for i in range(NTH)]
            RB = [Buf("R0"), Buf("R1")]
            MKB = Buf("maskb")
            EB = [Buf("E0"), Buf("E1")]
            MSB = [Buf("mTs%d" % i) for i in range(NTH)]
            HN2 = [MSB[0], MSB[0]]
            QRB = [MSB[1], MSB[1]]
            STB = [Buf("st%d" % i) for i in range(NTH)]
            CNB = [Buf("cnt%d" % i) for i in range(NTH)]
            SMQ = Buf("smq")
            scale = 0.125
            cnt_state = {"d": 0, "lg": 0, "e": 0}

            def scb(si, c0, c1):
                return SCB[si][c0 // 512:(c1 - 1) // 512 + 1]

            def tile_thread(ch, t, si, use_act):
                tl = t - ch * 4
                L = (t + 1) * 128
                nj = t + 1
                ALLSC = SCB[si][0:(L - 1) // 512 + 1]
                yield ("acquire", "slot%d" % si)
                if t < 2:
                    S.add("dve", lambda e, o=sc[si][:, 0:L]: e.memset(o, 0.0), writes=ALLSC, dur=0.3)
                for h in (range(8) if t >= 2 else ()):
                    j, hp = h // 2, h % 2
                    for s0 in range(0, L, 512):
                        n = min(512, L - s0)
                        b = cnt_state["d"] % 2
                        cnt_state["d"] += 1
                        kb = list({KTB[s0 // 512], KTB[(s0 + n - 1) // 512]})
                        S.add("pe", MM(bank(b, n), qiT[hp * 64:(hp + 1) * 64, j, tl * 128:(tl + 1) * 128], kiT2[hp * 64:(hp + 1) * 64, s0:s0 + n], True, True),
                              reads=[QITB] + kb, writes=[PB[b]], dur=n * 0.0009 + 0.05)
                        S.add("act", ACT(Rr[b][:, 0:n], bank(b, n), AF.Relu, scale=wabs[:, t, h:h + 1]), reads=[PB[b], WAB], writes=[RB[b]], dur=n * 0.00083 + 0.27)
                        cb_ = [SCB[si][s0 // 512]]
                        if h == 0:
                            S.add("dve", TS(sc[si][:, s0:s0 + n], Rr[b][:, 0:n], wsgn[:, t, 0:1], None, ALU.mult), reads=[RB[b], WAB], writes=cb_, dur=n * 0.00055 + 0.2)
                        else:
                            S.add("dve", STT(sc[si][:, s0:s0 + n], Rr[b][:, 0:n], wsgn[:, t, h:h + 1], sc[si][:, s0:s0 + n], ALU.mult, ALU.add),
                                  reads=[RB[b], WAB] + cb_, writes=cb_, dur=n * 0.00105 + 0.2)
                        yield n * 0.0011 + 0.3
                S.add("dve", TT(sc[si][:, L - 128:L], sc[si][:, L - 128:L], cmask, ALU.add), reads=[SCB[si][(L - 1) // 512], IDB], writes=[SCB[si][(L - 1) // 512]])
                if t == 4:
                    dbg("sc", sc[0], SCB[0])
                base = 64 + 64 * si
                lo = sm[:, base:base + 1]
                mid = sm[:, base + 1:base + 2]
                cnt = sm[:, base + 2:base + 3]
                tmp = sm[:, base + 3:base + 4]
                rng = sm[:, base + 4:base + 5]
                steps = sm[:, base + 8:base + 9 + NIT]
                SB_ = STB[si]
                CNB_ = CNB[si]
                junk = mTs[si]
                if t >= 2:
                    S.add("dve", lambda e, o=tmp, i=sc[si][:, 0:L]: e.tensor_reduce(out=o, in_=i, axis=AX.X, op=ALU.max), reads=ALLSC, writes=[SB_])
                    S.add("dve", lambda e, o=lo, i=sc[si][:, 0:L - 128]: e.tensor_reduce(out=o, in_=i, axis=AX.X, op=ALU.min), reads=ALLSC, writes=[SB_])
                    S.add("dve", TT(rng, tmp, lo, ALU.subtract), reads=[SB_], writes=[SB_])
                    S.add("dve", TS(steps, pow2[:, 0:NIT + 1], rng, None, ALU.mult), reads=[SB_, CST], writes=[SB_])
                    yield L * 0.0022 + 0.5
                    if use_act:
                        S.add("dve", STT(mid, lo, -1.0, steps[:, 0:1], ALU.mult, ALU.subtract), reads=[SB_], writes=[SB_])
                        thr = float(2 * TOPK - L)
                    else:
                        S.add("dve", TT(mid, lo, steps[:, 0:1], ALU.add), reads=[SB_], writes=[SB_])
                        thr = float(TOPK)
                    for it in range(NIT):
                        if use_act:
                            S.add("act", ACT(junk[:, 0:L], sc[si][:, 0:L], AF.Sign, bias=mid, accum_out=cnt), reads=ALLSC + [SB_], writes=[MSB[si], CNB_], dur=L * 0.00083 + 0.35)
                            yield L * 0.00085 + 0.45
                            S.add("dve", TS(tmp, cnt, thr, steps[:, it:it + 1], ALU.is_ge, ALU.mult), reads=[CNB_, SB_], writes=[SB_], dur=0.2)
                            S.add("dve", STT(mid, mid, steps[:, it + 1:it + 2], tmp, ALU.add, ALU.subtract), reads=[SB_], writes=[SB_], dur=0.2)
                        else:
                            S.add("dve", TS(junk[:, 0:L], sc[si][:, 0:L], mid, None, ALU.is_ge, ALU.add, accum=cnt), reads=ALLSC + [SB_], writes=[MSB[si], CNB_], dur=L * 0.00105 + 0.3)
                            yield L * 0.00105 + 0.3
                            S.add("dve", TS(tmp, cnt, thr, steps[:, it:it + 1], ALU.is_ge, ALU.mult), reads=[CNB_, SB_], writes=[SB_], dur=0.2)
                            S.add("dve", STT(mid, mid, steps[:, it + 1:it + 2], tmp, ALU.subtract, ALU.add), reads=[SB_], writes=[SB_], dur=0.2)
                        yield 0.5
                    if use_act:
                        S.add("dve", STT(lo, mid, -1.0, steps[:, NIT:NIT + 1], ALU.mult, ALU.subtract), reads=[SB_], writes=[SB_])
                    else:
                        S.add("dve", TT(lo, mid, steps[:, NIT:NIT + 1], ALU.subtract), reads=[SB_], writes=[SB_])
                else:
                    S.add("dve", lambda e, o=lo: e.memset(o, -1.0e29), writes=[SB_])
                if t == 4:
                    dbg("lo", sm[:, 64:66], [SB_])
                yield ("acquire", "mask")
                S.add("dve", TS(maskb[:, 0:L], sc[si][:, 0:L], lo, None, ALU.is_ge), reads=ALLSC + [SB_], writes=[MKB], dur=L * 0.00055 + 0.2)
                if t == 4:
                    dbg("mask", maskb, [MKB])
                for g0 in range(0, nj, 8):
                    g1 = min(nj, g0 + 8)
                    b = cnt_state["d"] % 2
                    cnt_state["d"] += 1
                    stg = bank_bf(b)
                    yield 1.2
                    for jj in range(g0, g1):
                        S.add("pe", TR(stg[:, (jj - g0) * 128:(jj - g0 + 1) * 128], maskb[:, jj * 128:(jj + 1) * 128], ident), reads=[MKB, IDB], writes=[PB[b]], dur=0.3)
                    S.add("act", ACT(mTs[si][:, g0 * 128:g1 * 128], stg[:, 0:(g1 - g0) * 128], AF.Copy), reads=[PB[b]], writes=[MSB[si]], dur=(g1 - g0) * 0.11 + 0.3)
                yield ("release", "mask")
                yield L * 0.0012 + 1.0
                yield ("acquire", "O")
                O = pp[:, 4 * 512:6 * 512]

                def qk(jj):
                    lb = (2, 3) if cnt_state["lg"] % 2 == 0 else (6, 7)
                    cnt_state["lg"] += 1
                    Lg = pp[:, lb[0] * 512:(lb[0] + 2) * 512]
                    for hp in range(2):
                        S.add("pe", MM(Lg[:, hp * 512:(hp + 1) * 512], kT2[hp * 64:(hp + 1) * 64, jj * 128:(jj + 1) * 128],
                                       qT[hp * 64:(hp + 1) * 64, :, tl * 128:(tl + 1) * 128], True, True),
                              reads=[KTB[jj // 4], QTB], writes=[PB[lb[0]], PB[lb[1]]], dur=0.5)
                    return Lg, lb
                def pv(jj, eb):
                    for hh in range(2):
                        S.add("pe", MM(O[:, hh * 512:(hh + 1) * 512], vaug[:, jj, :], Eb[eb][:, hh * 512:(hh + 1) * 512], jj == 0, jj == nj - 1),
                              reads=[VB[jj // 4], EB[eb]], writes=[PB[4], PB[5]], dur=0.45)
                nxt = qk(0)
                prev = None
                for jj in range(nj):
                    Lg, lb = nxt
                    if jj + 1 < nj:
                        nxt = qk(jj + 1)
                    eb = cnt_state["e"] % 2
                    cnt_state["e"] += 1
                    S.add("act", ACT(Eb[eb], Lg, AF.Exp, scale=scale), reads=[PB[lb[0]], PB[lb[1]]], writes=[EB[eb]], dur=1.05)
                    e3 = Eb[eb].rearrange("p (h t) -> p h t", h=8)
                    S.add("dve", TT(e3, e3, mTs[si][:, jj * 128:(jj + 1) * 128].unsqueeze(1).to_broadcast([128, 8, 128]), ALU.mult),
                          reads=[EB[eb], MSB[si]], writes=[EB[eb]], dur=0.65)
                    if prev is not None:
                        pv(*prev)
                    prev = (jj, eb)
                    yield 1.1
                pv(*prev)
                yield 0.5
                S.add("dve", lambda e, o=rcp[0:64, :], i=O[64:128, :]: e.reciprocal(out=o, in_=i), reads=[PB[4], PB[5]], writes=EB, dur=6.6)
                for hp in range(2):
                    S.add("dve", TT(yaT[hp * 64:(hp + 1) * 64, :, tl * 128:(tl + 1) * 128],
                                    O[0:64, hp * 512:(hp + 1) * 512].rearrange("p (j t) -> p j t", j=4),
                                    rcp[0:64, hp * 512:(hp + 1) * 512].rearrange("p (j t) -> p j t", j=4), ALU.mult),
                          reads=[PB[4], PB[5]] + EB, writes=[YAB], dur=0.65)
                yield ("release", "O")
                yield 8.0

            def run_threads(gens, max_active=2, offsets=None):
                pending = list(gens)
                active = []
                locks = {}
                while pending or active:
                    while pending and len(active) < max_active:
                        active.append([pending.pop(0), min([a[1] for a in active], default=0.0), None])
                    runnable = [a for a in active if a[2] is None or a[2] not in locks]
                    a = min(runnable, key=lambda z: z[1])
                    if a[2] is not None:
                        locks[a[2]] = a
                        a[2] = None
                    S.step_end = 0.0
                    try:
                        r = next(a[0])
                    except StopIteration:
                        active.remove(a)
                        for k in [k for k, v in locks.items() if v is a]:
                            del locks[k]
                        continue
                    if S.step_end > 0.0:
                        a[1] = max(a[1], S.step_end)
                    if isinstance(r, tuple):
                        if r[0] == "acquire":
                            if r[1] in locks:
                                a[2] = r[1]
                            else:
                                locks[r[1]] = a
                        else:
                            locks.pop(r[1])
                            for o in active:
                                if o[2] == r[1]:
                                    o[1] = max(o[1], a[1])

            def load_half(half):
                wga, wgab = load_w(lambda s: s.rearrange("p (c f) -> p c f", c=8),
                                   win_d[:, C_GA + half * 512:C_GA + (half + 1) * 512].rearrange("(c p) f -> p c f", p=128))
                wgp, wgpb = load_w(lambda s: s.rearrange("p (c f) -> p c f", c=8),
                                   win_d[:, C_GP + half * 512:C_GP + (half + 1) * 512].rearrange("(c p) f -> p c f", p=128))
                pk = slot_i[0] % 4
                slot_i[0] += 1
                wpa = slot_ap[pk][:, 0:2048].rearrange("p (j d) -> p j d", j=4)
                wpp = slot_ap[pk][:, 2048:4096].rearrange("p (j d) -> p j d", j=4)
                S.add("pool", DMA(wpa, pa_d[:, half * 512:(half + 1) * 512].rearrange("(j q) d -> q j d", q=128)), writes=[SLB[pk]], dma=True, key="slot%d" % pk)
                S.add("pool", DMA(wpp, pp_d[:, half * 512:(half + 1) * 512].rearrange("(j q) d -> q j d", q=128)), writes=[SLB[pk]], dma=True, key="slot%d" % pk)
                return wga, wgab, wgp, wgpb, wpa, wpp, SLB[pk]

            def load_q():
                a_ = load_w(lambda s: s.rearrange("p (c f) -> p c f", c=8), win_d[:, C_Q:C_Q + 512].rearrange("(c p) f -> p c f", p=128))
                b_ = load_w(lambda s: s.rearrange("p (c f) -> p c f", c=8), win_d[:, C_QI:C_QI + 512].rearrange("(c p) f -> p c f", p=128))
                return a_ + b_

            pass1()
            qw_next = load_q()
            HCT = [Buf("hTc%d" % i) for i in range(4)]
            for ch in range(4):
                wq, wqb, wqi, wqib = qw_next
                f32v = lambda off: view(Q0 + off, 2048, F32)
                QSET = [dict(z=sc[0][:, 0:512], a1=sc[0][:, 512:1024], a2=sc[0][:, 1024:1536], ZB=SCB[0][0:3], zb=0, tb=6, qr=qr4[0], QB=QR4[0], c0=32, lock="zqA"),
                        dict(z=sc[2][:, 0:512], a1=sc[2][:, 512:1024], a2=sc[2][:, 1024:1536], ZB=SCB[2][0:3], zb=2, tb=4, qr=qr4[1], QB=QR4[1], c0=48, lock="zqB")]
                ISET = [dict(z=sc[1][:, 0:512], a1=sc[1][:, 512:1024], a2=sc[1][:, 1024:1536], ZB=SCB[1][0:3], zb=1, tb=7, qr=qr4[2], QB=QR4[2], lock="ziA"),
                        dict(z=f32v(36864), a1=f32v(38912), a2=f32v(40960), ZB=[RB[0], RB[1], MKB], zb=3, tb=5, qr=qr4[3], QB=QR4[3], lock="ziB")]

                def rope8(t, z, a1, a2, dst, ZB, DB):
                    z4 = z.rearrange("p (h k i) -> p h k i", h=8, k=2)
                    zz = z.rearrange("p (h d) -> p h d", h=8)
                    a13 = a1.rearrange("p (h k i) -> p h k i", h=8, k=2)
                    a23 = a2.rearrange("p (h d) -> p h d", h=8)
                    cb = cosT[:, t, :].unsqueeze(1).unsqueeze(1).to_broadcast([128, 8, 2, 32])
                    S.add("dve", TT(a13, z4, cb, ALU.mult), reads=[ZB[0], TAB], writes=[ZB[1]], dur=0.65)
                    S.add("dve", TT(a23[:, :, 0:32], zz[:, :, 32:64], nsinT[:, t, :].unsqueeze(1).to_broadcast([128, 8, 32]), ALU.mult),
                          reads=[ZB[0], TAB], writes=[ZB[2]], dur=0.4)
                    S.add("dve", TT(a23[:, :, 32:64], zz[:, :, 0:32], sinT[:, t, :].unsqueeze(1).to_broadcast([128, 8, 32]), ALU.mult),
                          reads=[ZB[0], TAB], writes=[ZB[2]], dur=0.4)
                    yield None
                    S.add("dve", TT(dst, a1, a2, ALU.add), reads=[ZB[1], ZB[2], MSB[1]], writes=[DB], dur=0.65)

                def q_thread(t, tl, st):
                    yield ("acquire", st["lock"])
                    zq, t1, t2, ZQB = st["z"], st["a1"], st["a2"], st["ZB"]
                    for c in range(8):
                        S.add("pe", MM(bank(st["zb"]), hTc[:, c, tl * 128:(tl + 1) * 128], wq[:, c, :], c == 0, c == 7), reads=[HCT[tl], wqb], writes=[PB[st["zb"]]])
                    S.add("act", ACT(zq, bank(st["zb"]), AF.Copy), reads=[PB[st["zb"]]], writes=[ZQB[0]], dur=0.75)
                    yield None
                    S.add("dve", TT(t1, zq, zq, ALU.mult), reads=[ZQB[0]], writes=[ZQB[1]], dur=0.65)
                    ssq = sm[:, st["c0"]:st["c0"] + 8]
                    S.add("dve", lambda e, o=ssq, i=t1.rearrange("p (h d) -> p h d", h=8): e.tensor_reduce(out=o, in_=i, axis=AX.X, op=ALU.add),
                          reads=[ZQB[1]], writes=[st["QB"]], dur=0.7)
                    S.add("dve", TS(ssq, ssq, 1.0 / 64, EPS, ALU.mult, ALU.add), reads=[st["QB"]], writes=[st["QB"]], dur=0.2)
                    rq = sm[:, st["c0"] + 8:st["c0"] + 16]
                    S.add("pool", TT(rq, ssq, mhalf[:, 0:8], ALU.pow), reads=[st["QB"], CST], writes=[st["QB"]], dur=1.4)
                    yield None
                    z3 = zq.rearrange("p (h d) -> p h d", h=8)
                    S.add("dve", TT(z3, z3, rq.unsqueeze(2).to_broadcast([128, 8, 64]), ALU.mult), reads=[ZQB[0], st["QB"]], writes=[ZQB[0]], dur=0.65)
                    S.add("dve", TT(z3, z3, gq.unsqueeze(1).to_broadcast([128, 8, 64]), ALU.mult), reads=[ZQB[0], CST], writes=[ZQB[0]], dur=0.65)
                    yield None
                    yield from rope8(t, zq, t1, t2, st["qr"], ZQB, st["QB"])
                    tq = bank_bf(st["tb"])
                    for j in range(4):
                        S.add("pe", TR(tq[:, j * 128:(j + 1) * 128], st["qr"][:, j * 128:(j + 1) * 128], ident), reads=[st["QB"], IDB], writes=[PB[st["tb"]]], dur=0.3)
                    S.add("act", ACT(qT[:, :, tl * 128:(tl + 1) * 128], tq[:, 0:512].rearrange("p (j t) -> p j t", j=4), AF.Copy), reads=[PB[st["tb"]]], writes=[QTB], dur=0.75)
                    yield ("release", st["lock"])

                def qi_thread(t, tl, st):
                    yield ("acquire", st["lock"])
                    zqi, u1, u2, ZIB = st["z"], st["a1"], st["a2"], st["ZB"]
                    for c in range(8):
                        S.add("pe", MM(bank(st["zb"]), hTc[:, c, tl * 128:(tl + 1) * 128], wqi[:, c, :], c == 0, c == 7), reads=[HCT[tl], wqib], writes=[PB[st["zb"]]])
                    S.add("act", ACT(zqi, bank(st["zb"]), AF.Copy), reads=[PB[st["zb"]]], writes=[ZIB[0]], dur=0.75)
                    yield None
                    yield from rope8(t, zqi, u1, u2, st["qr"], ZIB, st["QB"])
                    tqi = bank_bf(st["tb"])
                    for j in range(4):
                        S.add("pe", TR(tqi[:, j * 128:(j + 1) * 128], st["qr"][:, j * 128:(j + 1) * 128], ident), reads=[st["QB"], IDB], writes=[PB[st["tb"]]], dur=0.3)
                    S.add("act", ACT(qiT[:, :, tl * 128:(tl + 1) * 128], tqi[:, 0:512].rearrange("p (j t) -> p j t", j=4), AF.Copy), reads=[PB[st["tb"]]], writes=[QITB], dur=0.75)
                    yield ("release", st["lock"])

                def nrm_thread():
                    for tl in range(4):
                        yield ("acquire", "nrm%d" % tl)
                    for tl in range(4):
                        t = ch * 4 + tl
                        norm_tile(t, hn2[t % 2], HN2[t % 2], 4 + (t % 2), 1, hTc[:, :, tl * 128:(tl + 1) * 128], [HCT[tl]] + ([HCB] if tl == 3 else []))
                        yield ("release", "nrm%d" % tl)

                def gated(gen, tl):
                    yield ("acquire", "nrm%d" % tl)
                    yield ("release", "nrm%d" % tl)
                    yield from gen

                ths = [nrm_thread()]
                for tl in range(4):
                    ths.append(gated(q_thread(ch * 4 + tl, tl, QSET[tl % 2]), tl))
                    ths.append(gated(qi_thread(ch * 4 + tl, tl, ISET[tl % 2]), tl))
                run_threads(ths, max_active=5)

                if ch == 1:
                    dbg("qT", qT, [QTB])
                    dbg("qiT", qiT, [QITB])
                half0 = load_half(0)
                run_threads([tile_thread(ch, ch * 4 + tl, (ch * 4 + tl) % NTH, use_act=True) for tl in (3, 2, 1, 0)], max_active=NTH, offsets=None)

                if ch == 1:
                    dbg("yaT", yaT, [YAB])
                mg = sc[0].bitcast(BF16).rearrange("p (c t) -> p c t", c=8)[:, :, 0:512] if False else view(Q0 + 12288, 8192, BF16).rearrange("p (c t) -> p c t", c=8)
                sg = [sc[1][:, 0:512], sc[1][:, 512:1024]]
                m1 = sc[1][:, 1024:1536]
                m2 = sc[1][:, 1536:2048]
                MGB = SCB[0]
                for half in range(2):
                    wga, wgab, wgp, wgpb, wpa, wpp, wpab = half0 if half == 0 else load_half(1)
                    wppb = wpab
                    for dl in range(4):
                        dmc = half * 4 + dl
                        for c in range(8):
                            S.add("pe", MM(bank(0), wga[:, c, dl * 128:(dl + 1) * 128], hTc[:, c, :], c == 0, c == 7), reads=[wgab, HCB] + HCT, writes=[PB[0]])
                        S.add("act", ACT(sg[0], bank(0), AF.Sigmoid), reads=[PB[0]], writes=[SCB[1][0]])
                        for j in range(4):
                            S.add("pe", MM(bank(1), wpa[:, j, dl * 128:(dl + 1) * 128], yaT[:, j, :], j == 0, j == 3), reads=[wpab, YAB], writes=[PB[1]])
                        S.add("dve", TT(m1, sg[0], bank(1), ALU.mult), reads=[SCB[1][0], PB[1]], writes=[SCB[1][2]])
                        for c in range(8):
                            S.add("pe", MM(bank(2), wgp[:, c, dl * 128:(dl + 1) * 128], hTc[:, c, :], c == 0, c == 7), reads=[wgpb, HCB] + HCT, writes=[PB[2]])
                        S.add("act", ACT(sg[1], bank(2), AF.Sigmoid), reads=[PB[2]], writes=[SCB[1][1]])
                        for g in range(4):
                            S.add("pe", MM(bank(3), wpp[:, g, dl * 128:(dl + 1) * 128], ypT[:, g, ch * 512:(ch + 1) * 512], g == 0, g == 3),
                                  reads=[wppb, YPB[ch]], writes=[PB[3]])
                        S.add("dve", TT(m2, sg[1], bank(3), ALU.mult), reads=[SCB[1][1], PB[3]], writes=[SCB[1][3]])
                        S.add("dve", TT(mg[:, dmc, :], m1, m2, ALU.add), reads=[SCB[1][2], SCB[1][3]], writes=[MGB[dmc // 2]])
                if ch == 1:
                    dbg("mg", mg, MGB)
                wos = [load_w(lambda s: s.rearrange("p (c d) -> p c d", c=8), wout_d[:, dh * 512:(dh + 1) * 512].rearrange("(c p) d -> p c d", p=128)) for dh in range(2)]
                if ch < 3:
                    qw_next = load_q()
                for dh in range(2):
                    wo, wob = wos[dh]
                    for tl in range(4):
                        t = ch * 4 + tl
                        b = 4 + (tl % 2)
                        for c in range(8):
                            S.add("pe", MM(bank(b), mg[:, c, tl * 128:(tl + 1) * 128], wo[:, c, :], c == 0, c == 7), reads=MGB + [wob], writes=[PB[b]])
                        xv = xs[:, t, dh * 512:(dh + 1) * 512]
                        S.add("dve", TT(xv, xv, bank(b), ALU.add), reads=[PB[b], XB[t]], writes=[XB[t]])

        stage = debug.get("_stage", 3) if isinstance(debug.get("_stage", 3), int) else 3
        ffn(0, final=False)
        dbg("x1", xs, XB)
        mix()
        dbg("x2", xs, XB)
        stores = ffn(1, final=True)
        S.emit(final_wait=stores)
    return nc


_NC_CACHE = {}


def _prep_consts(inp):
    cst = np.zeros((128, 256), np.float32)
    for k, nm in enumerate(["ffn1_norm", "mix_norm", "ffn2_norm"]):
        cst[:, k * 8:(k + 1) * 8] = np.asarray(inp[nm], np.float32).reshape(8, 128).T
    cst[:, 24:28] = np.asarray(inp["pool_b"], np.float32).reshape(4, 128).T
    cst[:, 28:32] = np.asarray(inp["pool_scale"], np.float32).reshape(4, 128).T
    cst[:, 32:96] = np.asarray(inp["q_norm"], np.float32).reshape(1, 64)
    cst[:, 96:160] = np.asarray(inp["k_norm"], np.float32).reshape(1, 64)
    cst[:, 160:192] = (10000.0 ** (-np.arange(0, 64, 2, dtype=np.float32) / 64)).astype(np.float32)[None, :]
    cst[:, 192:208] = (1.0 / np.arange(1, 17, dtype=np.float32))[None, :]
    cst[:, 208:224] = -0.5
    cst[:, 224:256] = (0.5 ** np.arange(1, 33, dtype=np.float64)).astype(np.float32)[None, :]
    return cst


def kernel(**inputs):
    inp = {k: np.asarray(v) for k, v in inputs.items()}
    if "nc" not in _NC_CACHE:
        _NC_CACHE["nc"] = build_nc()
    nc = _NC_CACHE["nc"]
    cst = _prep_consts(inp)
    shared = {
        "cst": cst,
        "ffn1_w1": np.ascontiguousarray(inp["ffn1_w1"][0], np.float32),
        "ffn1_w3": np.ascontiguousarray(inp["ffn1_w3"][0], np.float32),
        "ffn1_w2": np.ascontiguousarray(inp["ffn1_w2"][0], np.float32),
        "ffn2_w1": np.ascontiguousarray(inp["ffn2_w1"][0], np.float32),
        "ffn2_w3": np.ascontiguousarray(inp["ffn2_w3"][0], np.float32),
        "ffn2_w2": np.ascontiguousarray(inp["ffn2_w2"][0], np.float32),
        "w_in": np.ascontiguousarray(inp["w_in"][0], np.float32),
        "pool_w": np.ascontiguousarray(inp["pool_w"][0], np.float32),
        "proj_attn": np.ascontiguousarray(inp["proj_attn"][0], np.float32),
        "proj_pool": np.ascontiguousarray(inp["proj_pool"][0], np.float32),
        "w_out": np.ascontiguousarray(inp["w_out"][0], np.float32),
    }
    x = np.asarray(inp["x"], np.float32)
    pos = np.asarray(inp["positions"], np.int32)
    in_maps = []
    for b in range(8):
        m = dict(shared)
        m["x"] = np.ascontiguousarray(x[b])
        m["pos"] = np.ascontiguousarray(pos[b].reshape(NT, 128).T)
        in_maps.append(m)
    res = run_bass_kernel_spmd(nc, in_maps, core_ids=list(range(8)))
    return np.stack([np.asarray(r["out"], np.float32) for r in res.results], axis=0)
```

```python
import contextlib
import numpy as np
import concourse.bass as bass
import concourse.mybir as mybir
from concourse.bass_utils import run_bass_kernel_spmd

F32 = mybir.dt.float32
BF16 = mybir.dt.bfloat16
I32 = mybir.dt.int32
U8 = mybir.dt.uint8
ALU = mybir.AluOpType
AF = mybir.ActivationFunctionType
AX = mybir.AxisListType

D = 1024
S_LEN = 2048
NT = 16
DFF = 2816
NFC = 22
DIN = 3784
EPS = 1e-6
NIT = 14
TOPK = 256
NEG = -1.0e30
MASK_BIG = 30000.0
C_Q, C_K, C_V, C_QI, C_KI, C_WI, C_U, C_GA, C_GP = 0, 512, 576, 640, 1152, 1216, 1224, 1736, 2760


class Buf:
    __slots__ = ("name", "w", "r")

    def __init__(self, name=""):
        self.name = name
        self.w = None
        self.r = []


class Op:
    __slots__ = ("q", "fn", "deps", "dma", "key", "semval", "signal", "count", "id", "t_end")


class Sched:
    QUEUES = ("pe", "act", "dve", "pool", "sp")

    def __init__(self, nc):
        self.nc = nc
        self.ops = []
        self.dma_count = {}
        self.eng_free = {q: 0.0 for q in self.QUEUES}
        self.step_end = 0.0

    DEF_DUR = {"pe": 0.25, "act": 0.75, "dve": 0.45, "pool": 1.4, "sp": 0.1}
    LAT = 0.25

    def add(self, q, fn, reads=(), writes=(), dma=False, key=None, dur=None):
        op = Op()
        op.q, op.fn, op.dma, op.key = q, fn, dma, key
        op.id = len(self.ops)
        op.signal = False
        op.count = 0
        op.semval = 0
        deps = {}
        for b in reads:
            if b.w is not None:
                deps[b.w] = True
        for b in writes:
            if b.w is not None:
                deps.setdefault(b.w, False)
            for r in b.r:
                deps.setdefault(r, False)
        deps.pop(op.id, None)
        op.deps = deps
        if dma:
            if key is None:
                key = op.key = "dma%d" % op.id
            self.dma_count[key] = self.dma_count.get(key, 0) + 1
            op.semval = 16 * self.dma_count[key]
        t0 = self.eng_free[q]
        for did in deps:
            d = self.ops[did]
            if self._need(op, d, True):
                t0 = max(t0, d.t_end + self.LAT)
        if dma:
            op.t_end = t0 + (dur if dur is not None else 6.0)
            self.eng_free[q] = t0 + 0.15
        else:
            op.t_end = t0 + (dur if dur is not None else self.DEF_DUR[q])
            self.eng_free[q] = op.t_end
        self.step_end = max(self.step_end, op.t_end)
        for b in reads:
            b.r.append(op.id)
        for b in writes:
            b.w = op.id
            b.r = []
        self.ops.append(op)
        return op.id

    def _need(self, op, d, raw):
        if d.dma:
            return True
        if d.q == op.q and not op.dma:
            return op.q != "pe"
        return True

    def emit(self, final_wait=()):
        nc = self.nc
        ops = self.ops
        for op in ops:
            for did, raw in op.deps.items():
                if self._need(op, ops[did], raw):
                    ops[did].signal = True
        for fid in final_wait:
            ops[fid].signal = True
        cnt = {q: 0 for q in self.QUEUES}
        for op in ops:
            if not op.dma and op.signal:
                cnt[op.q] += 1
                op.count = cnt[op.q]
        with contextlib.ExitStack() as es:
            qsem = {q: es.enter_context(nc.semaphore("s_" + q)) for q in ("pe", "act", "dve", "pool")}
            dsem = {k: es.enter_context(nc.semaphore("d_" + str(k))) for k in self.dma_count}
            block = es.enter_context(nc.Block())
            byq = {q: [o for o in ops if o.q == q] for q in self.QUEUES}

            def run(q, eng):
                waited = {}

                def do_waits(need):
                    for nm, (sem, val) in need.items():
                        if waited.get(nm, 0) >= val:
                            continue
                        waited[nm] = val
                        eng.wait_ge(sem, val)

                for op in byq[q]:
                    need = {}
                    for did, raw in op.deps.items():
                        d = ops[did]
                        if not self._need(op, d, raw):
                            continue
                        if d.dma:
                            nm, sem, val = "d_" + str(d.key), dsem[d.key], d.semval
                        else:
                            nm, sem, val = "q_" + d.q, qsem[d.q], d.count
                        if nm not in need or need[nm][1] < val:
                            need[nm] = (sem, val)
                    do_waits(need)
                    ins = op.fn(eng)
                    if op.dma:
                        ins.then_inc(dsem[op.key], 16)
                    elif op.signal:
                        ins.then_inc(qsem[q], 1)
                if q == "sp":
                    need = {}
                    for fid in final_wait:
                        d = ops[fid]
                        nm = "d_" + str(d.key)
                        if nm not in need or need[nm][1] < d.semval:
                            need[nm] = (dsem[d.key], d.semval)
                    do_waits(need)

            @block.tensor
            def _(e):
                run("pe", e)

            @block.scalar
            def _(e):
                run("act", e)

            @block.vector
            def _(e):
                run("dve", e)

            @block.gpsimd
            def _(e):
                run("pool", e)

            @block.sync
            def _(e):
                run("sp", e)


def MM(out, lhsT, rhs, start, stop):
    return lambda e: e.matmul(out, lhsT=lhsT, rhs=rhs, start=start, stop=stop)


def TR(out, in_, ident):
    return lambda e: e.transpose(out=out, in_=in_, identity=ident)


def ACT(out, in_, func, **kw):
    return lambda e: e.activation(out=out, in_=in_, func=func, **kw)


def TT(out, in0, in1, op):
    return lambda e: e.tensor_tensor(out=out, in0=in0, in1=in1, op=op)


def TS(out, in0, s1, s2, op0, op1=None, accum=None):
    if accum is not None:
        return lambda e: e.tensor_scalar(out=out, in0=in0, scalar1=s1, scalar2=s2, op0=op0, op1=op1, accum_out=accum)
    if op1 is None:
        return lambda e: e.tensor_scalar(out=out, in0=in0, scalar1=s1, scalar2=None, op0=op0)
    return lambda e: e.tensor_scalar(out=out, in0=in0, scalar1=s1, scalar2=s2, op0=op0, op1=op1)


def STT(out, in0, scalar, in1, op0, op1):
    return lambda e: e.scalar_tensor_tensor(out=out, in0=in0, scalar=scalar, in1=in1, op0=op0, op1=op1)


def CP(out, in_):
    return lambda e: e.tensor_copy(out=out, in_=in_)


def DMA(out, in_):
    return lambda e: e.dma_start(out=out, in_=in_)


def build_nc(debug=None):
    debug = debug or {}
    nc = bass.Bass("TRN2", target_bir_lowering=False)
    dr = {}

    def din(name, shape, dt=F32):
        dr[name] = nc.dram_tensor(name, list(shape), dt, kind="ExternalInput").ap()
        return dr[name]

    x_d = din("x", [S_LEN, D])
    pos_d = din("pos", [128, NT], I32)
    cst_d = din("cst", [128, 256])
    w1_d = [din("ffn1_w1", [D, DFF]), din("ffn2_w1", [D, DFF])]
    w3_d = [din("ffn1_w3", [D, DFF]), din("ffn2_w3", [D, DFF])]
    w2_d = [din("ffn1_w2", [DFF, D]), din("ffn2_w2", [DFF, D])]
    win_d = din("w_in", [D, DIN])
    poolw_d = din("pool_w", [4, 128, 128])
    pa_d = din("proj_attn", [512, D])
    pp_d = din("proj_pool", [512, D])
    wout_d = din("w_out", [D, D])
    out_d = nc.dram_tensor("out", [S_LEN, D], F32, kind="ExternalOutput").ap()
    dbg_d = {k: nc.dram_tensor("dbg_" + k, list(shp), F32, kind="ExternalOutput").ap() for k, shp in debug.items()}

    S = Sched(nc)
    ARENA = 212000
    with contextlib.ExitStack() as es:
        arena = es.enter_context(nc.sbuf_tensor("arena", [128, ARENA], U8))
        pp = es.enter_context(nc.psum_tensor("pp", [128, 4096], F32))

        def view(off, nbytes, dt):
            return arena[:, off:off + nbytes].bitcast(dt)

        PB = [Buf("bank%d" % b) for b in range(8)]

        def bank(b, n=512):
            return pp[:, b * 512:b * 512 + n]

        def bank_bf(b):
            return pp[:, b * 512:(b + 1) * 512].bitcast(BF16)

        xs = view(0, 65536, F32).rearrange("p (t d) -> p t d", t=NT)
        XB = [Buf("x%d" % t) for t in range(NT)]
        slot_ap = [view(65536 + 8192 * k, 8192, BF16) for k in range(4)]
        SLB = [Buf("slot%d" % k) for k in range(4)]
        CB = 98304
        ident = view(CB, 256, BF16)
        identf = view(CB + 256, 512, F32)
        cmask = view(CB + 768, 512, F32)
        cst = view(CB + 1280, 1024, F32)
        cosT = view(CB + 2304, 2048, F32).rearrange("p (t i) -> p t i", t=NT)
        sinT = view(CB + 4352, 2048, F32).rearrange("p (t i) -> p t i", t=NT)
        nsinT = view(CB + 6400, 2048, F32).rearrange("p (t i) -> p t i", t=NT)
        posi = view(CB + 8448, 64, I32)
        posf = view(CB + 8512, 64, F32)
        ssb = view(CB + 8576, 64, F32)
        rstd = view(CB + 8640, 64, F32)
        wabs = view(CB + 8704, 512, F32).rearrange("p (t h) -> p t h", t=NT)
        wsgn = view(CB + 9216, 512, F32).rearrange("p (t h) -> p t h", t=NT)
        poolw = view(CB + 9728, 1024, BF16).rearrange("p (g d) -> p g d", g=4)
        bsc = view(CB + 10752, 16, F32)
        sm = view(CB + 10768, 2048, F32)
        ones_bf = view(CB + 12816, 128, BF16)
        negbig = view(CB + 12944, 4, F32)
        CST = Buf("cst")
        IDB = Buf("ident")
        TAB = Buf("tables")
        SSB = Buf("ss")
        RSB = Buf("rstd")
        WAB = Buf("wabs")
        SMB = Buf("sm")
        PWB = Buf("poolw")
        gcol = cst[:, 0:24].rearrange("p (k c) -> p k c", k=3)
        poolb = cst[:, 24:28]
        pools = cst[:, 28:32]
        gq = cst[:, 32:96]
        gk = cst[:, 96:160]
        invf = cst[:, 160:192]
        icnt = cst[:, 192:208]
        mhalf = cst[:, 208:224]
        pow2 = cst[:, 224:256]

        PH = 111616

        slot_i = [0]
        preload = {}

        def load_w(viewfn, dram_ap):
            k = slot_i[0] % 4
            slot_i[0] += 1
            v = viewfn(slot_ap[k])
            S.add("pool", DMA(v, dram_ap), writes=[SLB[k]], dma=True, key="slot%d" % k)
            return v, SLB[k]

        def dbg(name, ap, bufs):
            if name in dbg_d:
                q = "sp" if ap.dtype == F32 else "pool"
                S.add(q, DMA(dbg_d[name], ap), reads=bufs, dma=True)

        S.add("sp", DMA(cst, cst_d), writes=[CST], dma=True)
        S.add("sp", DMA(posi, pos_d), writes=[TAB], dma=True)
        for t in range(NT):
            S.add("sp", DMA(xs[:, t, :], x_d[t * 128:(t + 1) * 128, :]), writes=[XB[t]], dma=True, key="x%d" % t)
        S.add("pool", lambda e: e.memset(identf, 1.0), writes=[IDB])
        S.add("pool", lambda e: e.affine_select(out=identf, in_=identf, pattern=[[-1, 128]], compare_op=ALU.is_equal,
                                                fill=0.0, base=0, channel_multiplier=1), reads=[IDB], writes=[IDB])
        S.add("dve", CP(ident, identf), reads=[IDB], writes=[IDB])
        S.add("pool", lambda e: e.memset(cmask, 0.0), writes=[IDB])
        S.add("pool", lambda e: e.affine_select(out=cmask, in_=cmask, pattern=[[-1, 128]], compare_op=ALU.is_ge,
                                                fill=NEG, base=0, channel_multiplier=1), reads=[IDB], writes=[IDB])
        S.add("pool", lambda e: e.memset(ones_bf, 1.0), writes=[IDB])
        S.add("pool", lambda e: e.memset(negbig, -MASK_BIG), writes=[IDB])
        S.add("pool", DMA(poolw, poolw_d.rearrange("g c d -> c g d")), writes=[PWB], dma=True)
        def setup_tables():
            S.add("dve", CP(posf, posi), reads=[TAB], writes=[TAB])
            ang = sm.rearrange("p (t i) -> p t i", t=NT)
            S.add("dve", TT(ang, posf.unsqueeze(2).to_broadcast([128, NT, 32]), invf.unsqueeze(1).to_broadcast([128, NT, 32]), ALU.mult),
                  reads=[TAB, CST], writes=[SMB])
            PI = float(np.pi)
            TWO_PI = 2.0 * PI
            scr_i = view(PH + 86016, 2048, I32).rearrange("p (t i) -> p t i", t=NT)
            scr_f = view(PH + 88064, 2048, F32).rearrange("p (t i) -> p t i", t=NT)
            scr_c = view(PH + 90112, 2048, F32).rearrange("p (t i) -> p t i", t=NT)
            SCR = Buf("scr")

            def reduce_angle(dst, shift):
                S.add("dve", TS(dst, ang, shift, None, ALU.add), reads=[SMB], writes=[TAB])
                S.add("dve", TS(scr_f, dst, 1.0 / TWO_PI, None, ALU.mult), reads=[TAB], writes=[SCR])
                S.add("dve", CP(scr_i, scr_f), reads=[SCR], writes=[SCR])
                S.add("dve", CP(scr_f, scr_i), reads=[SCR], writes=[SCR])
                S.add("dve", STT(dst, scr_f, -TWO_PI, dst, ALU.mult, ALU.add), reads=[SCR, TAB], writes=[TAB])
                S.add("dve", TS(scr_c, dst, PI, -TWO_PI, ALU.is_gt, ALU.mult), reads=[TAB], writes=[SCR])
                S.add("dve", TT(dst, dst, scr_c, ALU.add), reads=[TAB, SCR], writes=[TAB])
                S.add("dve", TS(scr_c, dst, -PI, TWO_PI, ALU.is_lt, ALU.mult), reads=[TAB], writes=[SCR])
                S.add("dve", TT(dst, dst, scr_c, ALU.add), reads=[TAB, SCR], writes=[TAB])
                S.add("dve", TS(dst, dst, 3.141592, -3.141592, ALU.min, ALU.max), reads=[TAB], writes=[TAB])
            reduce_angle(sinT, 0.0)
            reduce_angle(cosT, 0.5 * PI)
            S.add("act", ACT(sinT, sinT, AF.Sin), reads=[TAB], writes=[TAB])
            S.add("act", ACT(cosT, cosT, AF.Sin), reads=[TAB], writes=[TAB])
            S.add("dve", TS(nsinT, sinT, -1.0, None, ALU.mult), reads=[TAB], writes=[TAB])

        S.add("dve", TT(bsc, poolb, pools, ALU.mult), reads=[CST], writes=[PWB])

        SSQ = [Buf("ssq%d" % i) for i in range(4)]
        RSQ = [Buf("rsq%d" % i) for i in range(4)]

        def norm_stats(junks, JBs, quarters=(0, 1, 2, 3)):
            for qq in quarters:
                for t in range(qq * 4, qq * 4 + 4):
                    S.add("act", ACT(junks[t % 2], xs[:, t, :], AF.Square, accum_out=ssb[:, t:t + 1]), reads=[XB[t]], writes=[JBs[t % 2], SSQ[qq]])
                sl_ = slice(qq * 4, qq * 4 + 4)
                S.add("dve", TS(ssb[:, sl_], ssb[:, sl_], 1.0 / D, EPS, ALU.mult, ALU.add), reads=[SSQ[qq]], writes=[SSQ[qq]])
                S.add("pool", TT(rstd[:, sl_], ssb[:, sl_], mhalf[:, 0:4], ALU.pow), reads=[SSQ[qq], CST], writes=[RSQ[qq]])

        def norm_tile(t, hn, HNB, tpb, gidx, dstT, dst_bufs):
            S.add("act", ACT(hn, xs[:, t, :], AF.Copy, scale=rstd[:, t:t + 1]), reads=[XB[t], RSQ[t // 4]], writes=[HNB])
            tp = bank_bf(tpb)
            for c in range(8):
                S.add("pe", TR(tp[:, c * 128:(c + 1) * 128], hn[:, c * 128:(c + 1) * 128], ident), reads=[HNB, IDB], writes=[PB[tpb]])
            S.add("dve", TT(dstT, tp.rearrange("p (c t) -> p c t", c=8), gcol[:, gidx, :].unsqueeze(2).to_broadcast([128, 8, 128]), ALU.mult),
                  reads=[PB[tpb], CST], writes=dst_bufs)

        def ffn(k, final):
            hT = view(PH, 32768, BF16).rearrange("p (c t) -> p c t", c=8)
            gT = view(PH + 32768, 45056, BF16).rearrange("p (f t) -> p f t", f=11)
            sl = [view(PH + 77824 + 1024 * i, 1024, BF16) for i in range(2)]
            junk = view(PH + 79872, 2048, BF16)
            hn = [view(PH + 81920 + 2048 * i, 2048, BF16) for i in range(2)]
            HTB = [Buf("hT%d" % t) for t in range(NT)]
            GTB = [Buf("gT%d" % q) for q in range(4)]
            SLBf = [Buf("sl0"), Buf("sl1")]
            JB = Buf("junk")
            HNB = [Buf("hn0"), Buf("hn1")]
            gidx = 0 if k == 0 else 2
            stores = []
            step = 0

            def phase_a_step(w1v, w1b, w3v, w3b, fl, fcl, q):
                nonlocal step
                b1 = step % 2
                b3 = 2 + step % 2
                step += 1
                hb = [HTB[q * 4 + i] for i in range(4)]
                for c in range(8):
                    S.add("pe", MM(bank(b1), w1v[:, c, fl * 128:(fl + 1) * 128], hT[:, c, q * 512:(q + 1) * 512], c == 0, c == 7),
                          reads=[w1b] + hb, writes=[PB[b1]])
                for c in range(8):
                    S.add("pe", MM(bank(b3), w3v[:, c, fl * 128:(fl + 1) * 128], hT[:, c, q * 512:(q + 1) * 512], c == 0, c == 7),
                          reads=[w3b] + hb, writes=[PB[b3]])
                S.add("act", ACT(sl[b1], bank(b1), AF.Silu), reads=[PB[b1]], writes=[SLBf[b1]])
                S.add("dve", TT(gT[:, fcl, q * 512:(q + 1) * 512], sl[b1], bank(b3), ALU.mult),
                      reads=[SLBf[b1], PB[b3]], writes=[GTB[q]])

            for fh in range(2):
                groups = [(0, 4), (4, 4), (8, 3)]
                for gi, (f0, nf) in enumerate(groups):
                    col0 = (fh * 11 + f0) * 128
                    vf = lambda s, nf=nf: s[:, 0:8 * nf * 128].rearrange("p (c f) -> p c f", c=8)
                    if fh == 0 and gi == 0 and k in preload:
                        w1v, w1b, w3v, w3b = preload.pop(k)
                    else:
                        w1v, w1b = load_w(vf, w1_d[k][:, col0:col0 + nf * 128].rearrange("(c p) f -> p c f", p=128))
                        w3v, w3b = load_w(vf, w3_d[k][:, col0:col0 + nf * 128].rearrange("(c p) f -> p c f", p=128))
                    if fh == 0 and gi == 0:
                        def nq(q):
                            norm_stats(hn, HNB, quarters=(q,))
                            for t in range(q * 4, q * 4 + 4):
                                norm_tile(t, hn[t % 2], HNB[t % 2], 4 + (t % 2), gidx, hT[:, :, t * 128:(t + 1) * 128], [HTB[t]])
                        nq(0)
                        for q in range(4):
                            if q < 3:
                                norm_stats(hn, HNB, quarters=(q + 1,))
                            for fl in range(nf):
                                phase_a_step(w1v, w1b, w3v, w3b, fl, f0 + fl, q)
                                if q < 3:
                                    t = (q + 1) * 4 + fl
                                    norm_tile(t, hn[t % 2], HNB[t % 2], 4 + (t % 2), gidx, hT[:, :, t * 128:(t + 1) * 128], [HTB[t]])
                        if k == 0:
                            setup_tables()
                    else:
                        for fl in range(nf):
                            for q in range(4):
                                phase_a_step(w1v, w1b, w3v, w3b, fl, f0 + fl, q)
                for tg in range(2):
                    for dh in range(2):
                        for (f0, nf) in groups:
                            row0 = (fh * 11 + f0) * 128
                            vf2 = lambda s, nf=nf: s[:, 0:nf * 512].rearrange("p (c d) -> p c d", c=nf)
                            w2v, w2b = load_w(vf2, w2_d[k][row0:row0 + nf * 128, dh * 512:(dh + 1) * 512].rearrange("(c p) d -> p c d", p=128))
                            for fl in range(nf):
                                fcl = f0 + fl
                                for ti in range(8):
                                    t = tg * 8 + ti
                                    S.add("pe", MM(bank(ti), gT[:, fcl, t * 128:(t + 1) * 128], w2v[:, fl, :], fcl == 0, fcl == 10),
                                          reads=[w2b, GTB[t // 4]], writes=[PB[ti]])
                        for ti in range(8):
                            t = tg * 8 + ti
                            xv = xs[:, t, dh * 512:(dh + 1) * 512]
                            S.add("dve", STT(xv, bank(ti), 0.5, xv, ALU.mult, ALU.add), reads=[PB[ti], XB[t]], writes=[XB[t]])
                            if final and fh == 1 and dh == 1:
                                stores.append(S.add("sp", DMA(out_d[t * 128:(t + 1) * 128, :], xs[:, t, :]), reads=[XB[t]], dma=True, key="st%d" % t))
            return stores

        def mix():
            P0 = PH
            kT2 = view(P0, 4096, BF16)
            kiT2 = view(P0 + 4096, 4096, BF16)
            vaug = view(P0 + 8192, 4096, BF16).rearrange("p (t d) -> p t d", t=NT)
            ypT = view(P0 + 12288, 16384, BF16).rearrange("p (g t) -> p g t", g=4)
            hTc = view(P0 + 28672, 8192, BF16).rearrange("p (c t) -> p c t", c=8)
            Q0 = P0 + 36864
            KTB = [Buf("kT%d" % c) for c in range(4)]
            VB = [Buf("v%d" % c) for c in range(4)]
            YPB = [Buf("yp%d" % c) for c in range(4)]
            HCB = Buf("hTc")


            ub = [view(Q0 + 8448 * i, 8448, F32).rearrange("p (g t) -> p g t", g=4) for i in range(2)]
            sA = view(Q0 + 16896, 8448, F32).rearrange("p (g t) -> p g t", g=4)
            sB = view(Q0 + 25344, 8448, F32).rearrange("p (g t) -> p g t", g=4)
            pooled = view(Q0 + 33792, 4096, BF16).rearrange("p (g t) -> p g t", g=4)
            zk = view(Q0 + 37888, 3200, F32).rearrange("p (t c) -> p t c", t=4)
            kt = [view(Q0 + 41088 + 1024 * i, 1024, F32).rearrange("p (t c) -> p t c", t=4) for i in range(4)]
            kk = [view(Q0 + 45184 + 1024 * i, 1024, BF16).rearrange("p (t c) -> p t c", t=4) for i in range(2)]
            hn = [view(Q0 + 47232 + 2048 * i, 2048, BF16) for i in range(2)]
            UBB = [Buf("ub0"), Buf("ub1")]
            SAB, SBB, PLB, ZKB, KKB = Buf("sA"), Buf("sB"), Buf("pooled"), Buf("zk"), Buf("kk")
            KTMP = Buf("ktmp")
            HNB = [Buf("hn0"), Buf("hn1")]
            norm_stats(hn, HNB)

            def vkv(s):
                return s[:, 0:8 * 200].rearrange("p (c f) -> p c f", c=8)
            kslot = slot_i[0] % 4
            slot_i[0] += 1
            wkv = vkv(slot_ap[kslot])
            wkvb = SLB[kslot]
            for (dst0, src0, n) in [(0, C_K, 64), (64, C_KI, 64), (128, C_V, 64), (192, C_WI, 8)]:
                S.add("pool", DMA(wkv[:, :, dst0:dst0 + n], win_d[:, src0:src0 + n].rearrange("(c p) f -> p c f", p=128)),
                      writes=[wkvb], dma=True, key="kv%d" % dst0)
            wu, wub = load_w(lambda s: s.rearrange("p (c f) -> p c f", c=8), win_d[:, C_U:C_U + 512].rearrange("(c p) f -> p c f", p=128))

            zk2 = [zk, view(Q0 + 51328, 3200, F32).rearrange("p (t c) -> p t c", t=4)]
            kt2 = [kt, [view(Q0 + 54528 + 1024 * i, 1024, F32).rearrange("p (t c) -> p t c", t=4) for i in range(4)]]
            kk2 = [kk, [view(Q0 + 58624 + 1024 * i, 1024, BF16).rearrange("p (t c) -> p t c", t=4) for i in range(2)]]
            ZKB2, KTMP2, KKB2, SMK = [ZKB, Buf("zk1")], [KTMP, Buf("ktmp1")], [KKB, Buf("kk1")], [Buf("smk0"), Buf("smk1")]

            def p1_thread(ch):
                p = ch % 2
                zk_, kt_, kk_ = zk2[p], kt2[p], kk2[p]
                ZKB_, KTMP_, KKB_, SMK_ = ZKB2[p], KTMP2[p], KKB2[p], SMK[p]
                cur, prv = ub[ch % 2], ub[(ch + 1) % 2]
                CUB, PVB = UBB[ch % 2], UBB[(ch + 1) % 2]
                yield ("acquire", "p1set%d" % p)
                yield ("acquire", "hTc")
                for tl in range(4):
                    t = ch * 4 + tl
                    norm_tile(t, hn[t % 2], HNB[t % 2], 4 + (t % 2), 1, hTc[:, :, tl * 128:(tl + 1) * 128], [HCB])
                    yield None
                for tl in range(4):
                    b = tl % 2
                    for c in range(8):
                        S.add("pe", MM(bank(b, 200), hTc[:, c, tl * 128:(tl + 1) * 128], wkv[:, c, :], c == 0, c == 7),
                              reads=[HCB, wkvb], writes=[PB[b]], dur=0.12)
                    S.add("act", ACT(zk_[:, tl, :], bank(b, 200), AF.Copy), reads=[PB[b]], writes=[ZKB_], dur=0.45)
                    yield None
                for g in range(4):
                    b = 2 + (g % 2)
                    for c in range(8):
                        S.add("pe", MM(bank(b), wu[:, c, g * 128:(g + 1) * 128], hTc[:, c, :], c == 0, c == 7),
                              reads=[HCB, wub], writes=[PB[b]], dur=0.25)
                    S.add("act", ACT(cur[:, g, 16:528], bank(b), AF.Copy), reads=[PB[b]], writes=[CUB], dur=0.75)
                    yield None
                yield ("release", "hTc")
                if ch > 0:
                    S.add("dve", CP(cur[:, :, 0:16], prv[:, :, 512:528]), reads=[PVB], writes=[CUB], dur=0.2)
                tsl = slice(ch * 4, ch * 4 + 4)
                kraw = zk_[:, :, 0:64]
                S.add("dve", TT(kt_[0], kraw, kraw, ALU.mult), reads=[ZKB_], writes=[KTMP_], dur=0.4)
                ssk = sm[:, 8 * p:8 * p + 4]
                S.add("dve", lambda e, o=ssk, i=kt_[0]: e.tensor_reduce(out=o, in_=i, axis=AX.X, op=ALU.add), reads=[KTMP_], writes=[SMK_], dur=0.4)
                S.add("dve", TS(ssk, ssk, 1.0 / 64, EPS, ALU.mult, ALU.add), reads=[SMK_], writes=[SMK_], dur=0.2)
                rk = sm[:, 8 * p + 4:8 * p + 8]
                S.add("pool", TT(rk, ssk, mhalf[:, 0:4], ALU.pow), reads=[SMK_, CST], writes=[SMK_], dur=1.4)
                yield None
                S.add("dve", TT(kt_[0], kraw, rk.unsqueeze(2).to_broadcast([128, 4, 64]), ALU.mult), reads=[ZKB_, SMK_], writes=[KTMP_], dur=0.4)
                S.add("dve", TT(kt_[0], kt_[0], gk.unsqueeze(1).to_broadcast([128, 4, 64]), ALU.mult), reads=[KTMP_, CST], writes=[KTMP_], dur=0.4)
                yield None

                def rope4(src, dst_list, RB_, WBs):
                    s4 = src.rearrange("p t (h i) -> p t h i", h=2)
                    cb = cosT[:, tsl, :].unsqueeze(2).to_broadcast([128, 4, 2, 32])
                    S.add("dve", TT(kt_[1].rearrange("p t (h i) -> p t h i", h=2), s4, cb, ALU.mult), reads=[RB_, TAB], writes=[KTMP_], dur=0.4)
                    S.add("dve", TT(kt_[2][:, :, 0:32], src[:, :, 32:64], nsinT[:, tsl, :], ALU.mult), reads=[RB_, TAB], writes=[KTMP_], dur=0.3)
                    S.add("dve", TT(kt_[2][:, :, 32:64], src[:, :, 0:32], sinT[:, tsl, :], ALU.mult), reads=[RB_, TAB], writes=[KTMP_], dur=0.3)
                    for d in dst_list:
                        S.add("dve", TT(d, kt_[1], kt_[2], ALU.add), reads=[KTMP_], writes=WBs, dur=0.4)
                rope4(kt_[0], [kk_[0][:, :, 0:64], kk_[0][:, :, 64:128]], KTMP_, [KKB_])
                yield None
                S.add("dve", CP(kt_[3], zk_[:, :, 64:128]), reads=[ZKB_], writes=[KTMP_], dur=0.3)
                rope4(kt_[3], [kk_[1][:, :, 0:64], kk_[1][:, :, 64:128]], KTMP_, [KKB_])
                yield None
                S.add("dve", CP(vaug[:, tsl, 0:64], zk_[:, :, 128:192]), reads=[ZKB_], writes=[VB[ch]], dur=0.3)
                S.add("dve", lambda e, o=vaug[:, tsl, 64:128]: e.memset(o, 1.0), writes=[VB[ch]], dur=0.2)
                S.add("dve", STT(wabs[:, tsl, :], zk_[:, :, 192:200], -1.0, zk_[:, :, 192:200], ALU.mult, ALU.max), reads=[ZKB_], writes=[WAB], dur=0.2)
                S.add("dve", TS(wsgn[:, tsl, :], zk_[:, :, 192:200], 0.0, 2.0, ALU.is_ge, ALU.mult), reads=[ZKB_], writes=[WAB], dur=0.2)
                S.add("dve", TS(wsgn[:, tsl, :], wsgn[:, tsl, :], -1.0, None, ALU.add), reads=[WAB], writes=[WAB], dur=0.2)
                yield None
                tpk = bank_bf(6)
                for tl in range(4):
                    S.add("pe", TR(tpk[:, tl * 128:(tl + 1) * 128], kk_[0][:, tl, :], ident), reads=[KKB_, IDB], writes=[PB[6]], dur=0.3)
                    S.add("pe", TR(tpk[:, 512 + tl * 128:512 + (tl + 1) * 128], kk_[1][:, tl, :], ident), reads=[KKB_, IDB], writes=[PB[6]], dur=0.3)
                S.add("act", ACT(kT2[:, ch * 512:(ch + 1) * 512], tpk[:, 0:512], AF.Copy), reads=[PB[6]], writes=[KTB[ch]], dur=0.75)
                S.add("act", ACT(kiT2[:, ch * 512:(ch + 1) * 512], tpk[:, 512:1024], AF.Copy), reads=[PB[6]], writes=[KTB[ch]], dur=0.75)
                yield None
                yield ("acquire", "pool")
                u = cur
                S.add("dve", TT(sA[:, :, 1:528], u[:, :, 1:528], u[:, :, 0:527], ALU.add), reads=[CUB], writes=[SAB], dur=2.3)
                yield None
                S.add("dve", TT(sB[:, 1:4, 3:528], sA[:, 1:4, 3:528], sA[:, 1:4, 1:526], ALU.add), reads=[SAB], writes=[SBB], dur=1.8)
                yield None
                S.add("dve", TT(sA[:, 2:4, 7:528], sB[:, 2:4, 7:528], sB[:, 2:4, 3:524], ALU.add), reads=[SBB], writes=[SAB], dur=1.2)
                S.add("dve", TT(sB[:, 3:4, 15:528], sA[:, 3:4, 15:528], sA[:, 3:4, 7:520], ALU.add), reads=[SAB], writes=[SBB], dur=0.7)
                yield None
                srcs = [(sA, SAB, 0, 2), (sB, SBB, 1, 4), (sA, SAB, 2, 8), (sB, SBB, 3, 16)]
                for (sv, svb, g, w) in srcs:
                    S.add("dve", STT(pooled[:, g, :], sv[:, g, 16:528], 1.0 / w, u[:, g, 16:528], ALU.mult, ALU.subtract),
                          reads=[svb, CUB], writes=[PLB], dur=0.7)
                    if ch == 0:
                        n = w - 1
                        S.add("dve", TT(sm[:, 16:16 + n], sv[:, g, 16:16 + n], icnt[:, 0:n], ALU.mult), reads=[svb, CST], writes=[SMB], dur=0.2)
                        S.add("dve", TT(pooled[:, g, 0:n], sm[:, 16:16 + n], u[:, g, 16:16 + n], ALU.subtract), reads=[SMB, CUB], writes=[PLB], dur=0.2)
                    yield None
                for g in range(4):
                    b = g % 2
                    S.add("pe", MM(bank(b), poolw[:, g, :], pooled[:, g, :], True, True), reads=[PWB, PLB], writes=[PB[b]], dur=0.25)
                    S.add("act", ACT(ypT[:, g, ch * 512:(ch + 1) * 512], bank(b), AF.Identity, bias=bsc[:, g:g + 1], scale=pools[:, g:g + 1]),
                          reads=[PB[b], PWB, CST], writes=[YPB[ch]], dur=0.75)
                yield ("release", "pool")

            def pass1():
                S.add("dve", lambda e: e.memset(ub[0][:, :, 0:16], 0.0), writes=[UBB[0]])
                run_threads([p1_thread(ch) for ch in range(4)], max_active=2)
            dbg("kT2", kT2, KTB)
            dbg("kiT2", kiT2, KTB)
            dbg("ypT", ypT, YPB)
            dbg("vaug", vaug, VB)
            dbg("wabs", wabs, [WAB])
            dbg("wsgn", wsgn, [WAB])

            qT = view(Q0, 4096, BF16).rearrange("p (j t) -> p j t", j=4)
            qiT = view(Q0 + 4096, 4096, BF16).rearrange("p (j t) -> p j t", j=4)
            yaT = view(Q0 + 8192, 4096, BF16).rearrange("p (j t) -> p j t", j=4)
            NTH = 3
            sc = [view(Q0 + 12288 + 8192 * i, 8192, F32) for i in range(NTH)]
            Rr = [view(Q0 + 36864 + 2048 * i, 2048, F32) for i in range(2)]
            maskb = view(Q0 + 40960, 4096, BF16)
            Eb = [view(Q0 + 45056 + 2048 * i, 2048, BF16) for i in range(2)]
            rcp = view(Q0 + 45056, 4096, F32)
            mTs = [view(Q0 + 49152 + 4096 * i, 4096, BF16) for i in range(NTH)]
            hn2 = [view(Q0 + 49152 + 2048 * i, 2048, BF16) for i in range(2)]
            qr4 = [view(Q0 + 53248 + 1024 * i, 1024, BF16) for i in range(4)]
            QR4 = [Buf("qr%d" % i) for i in range(4)]
            QTB, QITB, YAB = Buf("qT"), Buf("qiT"), Buf("yaT")
            SCB = [[Buf("sc%d_%d" % (i, c)) for c in range(4)] for i in range(NTH)]
            RB = [Buf("R0"), Buf("R1")]
            MKB = Buf("maskb")
            EB = [Buf("E0"), Buf("E1")]
            MSB = [Buf("mTs%d" % i) for i in range(NTH)]
            HN2 = [MSB[0], MSB[0]]
            QRB = [MSB[1], MSB[1]]
            STB = [Buf("st%d" % i) for i in range(NTH)]
            CNB = [Buf("cnt%d" % i) for i in range(NTH)]
            SMQ = Buf("smq")
            scale = 0.125
            cnt_state = {"d": 0, "lg": 0, "e": 0}

            def scb(si, c0, c1):
                return SCB[si][c0 // 512:(c1 - 1) // 512 + 1]

            def tile_thread(ch, t, si, use_act):
                tl = t - ch * 4
                L = (t + 1) * 128
                nj = t + 1
                ALLSC = SCB[si][0:(L - 1) // 512 + 1]
                yield ("acquire", "slot%d" % si)
                if t < 2:
                    S.add("dve", lambda e, o=sc[si][:, 0:L]: e.memset(o, 0.0), writes=ALLSC, dur=0.3)
                for h in (range(8) if t >= 2 else ()):
                    j, hp = h // 2, h % 2
                    for s0 in range(0, L, 512):
                        n = min(512, L - s0)
                        b = cnt_state["d"] % 2
                        cnt_state["d"] += 1
                        kb = list({KTB[s0 // 512], KTB[(s0 + n - 1) // 512]})
                        S.add("pe", MM(bank(b, n), qiT[hp * 64:(hp + 1) * 64, j, tl * 128:(tl + 1) * 128], kiT2[hp * 64:(hp + 1) * 64, s0:s0 + n], True, True),
                              reads=[QITB] + kb, writes=[PB[b]], dur=n * 0.0009 + 0.05)
                        S.add("act", ACT(Rr[b][:, 0:n], bank(b, n), AF.Relu, scale=wabs[:, t, h:h + 1]), reads=[PB[b], WAB], writes=[RB[b]], dur=n * 0.00083 + 0.27)
                        cb_ = [SCB[si][s0 // 512]]
                        if h == 0:
                            S.add("dve", TS(sc[si][:, s0:s0 + n], Rr[b][:, 0:n], wsgn[:, t, 0:1], None, ALU.mult), reads=[RB[b], WAB], writes=cb_, dur=n * 0.00055 + 0.2)
                        else:
                            S.add("dve", STT(sc[si][:, s0:s0 + n], Rr[b][:, 0:n], wsgn[:, t, h:h + 1], sc[si][:, s0:s0 + n], ALU.mult, ALU.add),
                                  reads=[RB[b], WAB] + cb_, writes=cb_, dur=n * 0.00105 + 0.2)
                        yield n * 0.0011 + 0.3
                S.add("dve", TT(sc[si][:, L - 128:L], sc[si][:, L - 128:L], cmask, ALU.add), reads=[SCB[si][(L - 1) // 512], IDB], writes=[SCB[si][(L - 1) // 512]])
                if t == 4:
                    dbg("sc", sc[0], SCB[0])
                base = 64 + 64 * si
                lo = sm[:, base:base + 1]
                mid = sm[:, base + 1:base + 2]
                cnt = sm[:, base + 2:base + 3]
                tmp = sm[:, base + 3:base + 4]
                rng = sm[:, base + 4:base + 5]
                steps = sm[:, base + 8:base + 9 + NIT]
                SB_ = STB[si]
                CNB_ = CNB[si]
                junk = mTs[si]
                if t >= 2:
                    S.add("dve", lambda e, o=tmp, i=sc[si][:, 0:L]: e.tensor_reduce(out=o, in_=i, axis=AX.X, op=ALU.max), reads=ALLSC, writes=[SB_])
                    S.add("dve", lambda e, o=lo, i=sc[si][:, 0:L - 128]: e.tensor_reduce(out=o, in_=i, axis=AX.X, op=ALU.min), reads=ALLSC, writes=[SB_])
                    S.add("dve", TT(rng, tmp, lo, ALU.subtract), reads=[SB_], writes=[SB_])
                    S.add("dve", TS(steps, pow2[:, 0:NIT + 1], rng, None, ALU.mult), reads=[SB_, CST], writes=[SB_])
                    yield L * 0.0022 + 0.5
                    if use_act:
                        S.add("dve", STT(mid, lo, -1.0, steps[:, 0:1], ALU.mult, ALU.subtract), reads=[SB_], writes=[SB_])
                        thr = float(2 * TOPK - L)
                    else:
                        S.add("dve", TT(mid, lo, steps[:, 0:1], ALU.add), reads=[SB_], writes=[SB_])
                        thr = float(TOPK)
                    for it in range(NIT):
                        if use_act:
                            S.add("act", ACT(junk[:, 0:L], sc[si][:, 0:L], AF.Sign, bias=mid, accum_out=cnt), reads=ALLSC + [SB_], writes=[MSB[si], CNB_], dur=L * 0.00083 + 0.35)
                            yield L * 0.00085 + 0.45
                            S.add("dve", TS(tmp, cnt, thr, steps[:, it:it + 1], ALU.is_ge, ALU.mult), reads=[CNB_, SB_], writes=[SB_], dur=0.2)
                            S.add("dve", STT(mid, mid, steps[:, it + 1:it + 2], tmp, ALU.add, ALU.subtract), reads=[SB_], writes=[SB_], dur=0.2)
                        else:
                            S.add("dve", TS(junk[:, 0:L], sc[si][:, 0:L], mid, None, ALU.is_ge, ALU.add, accum=cnt), reads=ALLSC + [SB_], writes=[MSB[si], CNB_], dur=L * 0.00105 + 0.3)
                            yield L * 0.00105 + 0.3
                            S.add("dve", TS(tmp, cnt, thr, steps[:, it:it + 1], ALU.is_ge, ALU.mult), reads=[CNB_, SB_], writes=[SB_], dur=0.2)
                            S.add("dve", STT(mid, mid, steps[:, it + 1:it + 2], tmp, ALU.subtract, ALU.add), reads=[SB_], writes=[SB_], dur=0.2)
                        yield 0.5
                    if use_act:
                        S.add("dve", STT(lo, mid, -1.0, steps[:, NIT:NIT + 1], ALU.mult, ALU.subtract), reads=[SB_], writes=[SB_])
                    else:
                        S.add("dve", TT(lo, mid, steps[:, NIT:NIT + 1], ALU.subtract), reads=[SB_], writes=[SB_])
                else:
                    S.add("dve", lambda e, o=lo: e.memset(o, -1.0e29), writes=[SB_])
                if t == 4:
                    dbg("lo", sm[:, 64:66], [SB_])
                yield ("acquire", "mask")
                S.add("dve", TS(maskb[:, 0:L], sc[si][:, 0:L], lo, None, ALU.is_ge), reads=ALLSC + [SB_], writes=[MKB], dur=L * 0.00055 + 0.2)
                if t == 4:
                    dbg("mask", maskb, [MKB])
                for g0 in range(0, nj, 8):
                    g1 = min(nj, g0 + 8)
                    b = cnt_state["d"] % 2
                    cnt_state["d"] += 1
                    stg = bank_bf(b)
                    yield 1.2
                    for jj in range(g0, g1):
                        S.add("pe", TR(stg[:, (jj - g0) * 128:(jj - g0 + 1) * 128], maskb[:, jj * 128:(jj + 1) * 128], ident), reads=[MKB, IDB], writes=[PB[b]], dur=0.3)
                    S.add("act", ACT(mTs[si][:, g0 * 128:g1 * 128], stg[:, 0:(g1 - g0) * 128], AF.Copy), reads=[PB[b]], writes=[MSB[si]], dur=(g1 - g0) * 0.11 + 0.3)
                yield ("release", "mask")
                yield L * 0.0012 + 1.0
                yield ("acquire", "O")
                O = pp[:, 4 * 512:6 * 512]

                def qk(jj):
                    lb = (2, 3) if cnt_state["lg"] % 2 == 0 else (6, 7)
                    cnt_state["lg"] += 1
                    Lg = pp[:, lb[0] * 512:(lb[0] + 2) * 512]
                    for hp in range(2):
                        S.add("pe", MM(Lg[:, hp * 512:(hp + 1) * 512], kT2[hp * 64:(hp + 1) * 64, jj * 128:(jj + 1) * 128],
                                       qT[hp * 64:(hp + 1) * 64, :, tl * 128:(tl + 1) * 128], True, True),
                              reads=[KTB[jj // 4], QTB], writes=[PB[lb[0]], PB[lb[1]]], dur=0.5)
                    return Lg, lb
                def pv(jj, eb):
                    for hh in range(2):
                        S.add("pe", MM(O[:, hh * 512:(hh + 1) * 512], vaug[:, jj, :], Eb[eb][:, hh * 512:(hh + 1) * 512], jj == 0, jj == nj - 1),
                              reads=[VB[jj // 4], EB[eb]], writes=[PB[4], PB[5]], dur=0.45)
                nxt = qk(0)
                prev = None
                for jj in range(nj):
                    Lg, lb = nxt
                    if jj + 1 < nj:
                        nxt = qk(jj + 1)
                    eb = cnt_state["e"] % 2
                    cnt_state["e"] += 1
                    S.add("act", ACT(Eb[eb], Lg, AF.Exp, scale=scale), reads=[PB[lb[0]], PB[lb[1]]], writes=[EB[eb]], dur=1.05)
                    e3 = Eb[eb].rearrange("p (h t) -> p h t", h=8)
                    S.add("dve", TT(e3, e3, mTs[si][:, jj * 128:(jj + 1) * 128].unsqueeze(1).to_broadcast([128, 8, 128]), ALU.mult),
                          reads=[EB[eb], MSB[si]], writes=[EB[eb]], dur=0.65)
                    if prev is not None:
                        pv(*prev)
                    prev = (jj, eb)
                    yield 1.1
                pv(*prev)
                yield 0.5
                S.add("dve", lambda e, o=rcp[0:64, :], i=O[64:128, :]: e.reciprocal(out=o, in_=i), reads=[PB[4], PB[5]], writes=EB, dur=6.6)
                for hp in range(2):
                    S.add("dve", TT(yaT[hp * 64:(hp + 1) * 64, :, tl * 128:(tl + 1) * 128],
                                    O[0:64, hp * 512:(hp + 1) * 512].rearrange("p (j t) -> p j t", j=4),
                                    rcp[0:64, hp * 512:(hp + 1) * 512].rearrange("p (j t) -> p j t", j=4), ALU.mult),
                          reads=[PB[4], PB[5]] + EB, writes=[YAB], dur=0.65)
                yield ("release", "O")
                yield 8.0

            def run_threads(gens, max_active=2, offsets=None):
                pending = list(gens)
                active = []
                locks = {}
                while pending or active:
                    while pending and len(active) < max_active:
                        active.append([pending.pop(0), min([a[1] for a in active], default=0.0), None])
                    runnable = [a for a in active if a[2] is None or a[2] not in locks]
                    a = min(runnable, key=lambda z: z[1])
                    if a[2] is not None:
                        locks[a[2]] = a
                        a[2] = None
                    S.step_end = 0.0
                    try:
                        r = next(a[0])
                    except StopIteration:
                        active.remove(a)
                        for k in [k for k, v in locks.items() if v is a]:
                            del locks[k]
                        continue
                    if S.step_end > 0.0:
                        a[1] = max(a[1], S.step_end)
                    if isinstance(r, tuple):
                        if r[0] == "acquire":
                            if r[1] in locks:
                                a[2] = r[1]
                            else:
                                locks[r[1]] = a
                        else:
                            locks.pop(r[1])
                            for o in active:
                                if o[2] == r[1]:
                                    o[1] = max(o[1], a[1])

            def load_half(half):
                wga, wgab = load_w(lambda s: s.rearrange("p (c f) -> p c f", c=8),
                                   win_d[:, C_GA + half * 512:C_GA + (half + 1) * 512].rearrange("(c p) f -> p c f", p=128))
                wgp, wgpb = load_w(lambda s: s.rearrange("p (c f) -> p c f", c=8),
                                   win_d[:, C_GP + half * 512:C_GP + (half + 1) * 512].rearrange("(c p) f -> p c f", p=128))
                pk = slot_i[0] % 4
                slot_i[0] += 1
                wpa = slot_ap[pk][:, 0:2048].rearrange("p (j d) -> p j d", j=4)
                wpp = slot_ap[pk][:, 2048:4096].rearrange("p (j d) -> p j d", j=4)
                S.add("pool", DMA(wpa, pa_d[:, half * 512:(half + 1) * 512].rearrange("(j q) d -> q j d", q=128)), writes=[SLB[pk]], dma=True, key="slot%d" % pk)
                S.add("pool", DMA(wpp, pp_d[:, half * 512:(half + 1) * 512].rearrange("(j q) d -> q j d", q=128)), writes=[SLB[pk]], dma=True, key="slot%d" % pk)
                return wga, wgab, wgp, wgpb, wpa, wpp, SLB[pk]

            def load_q():
                a_ = load_w(lambda s: s.rearrange("p (c f) -> p c f", c=8), win_d[:, C_Q:C_Q + 512].rearrange("(c p) f -> p c f", p=128))
                b_ = load_w(lambda s: s.rearrange("p (c f) -> p c f", c=8), win_d[:, C_QI:C_QI + 512].rearrange("(c p) f -> p c f", p=128))
                return a_ + b_

            qw_next = load_q()
            pass1()
            HCT = [Buf("hTc%d" % i) for i in range(4)]
            for ch in range(4):
                wq, wqb, wqi, wqib = qw_next
                f32v = lambda off: view(Q0 + off, 2048, F32)
                QSET = [dict(z=sc[0][:, 0:512], a1=sc[0][:, 512:1024], a2=sc[0][:, 1024:1536], ZB=SCB[0][0:3], zb=0, tb=6, qr=qr4[0], QB=QR4[0], c0=32, lock="zqA"),
                        dict(z=sc[2][:, 0:512], a1=sc[2][:, 512:1024], a2=sc[2][:, 1024:1536], ZB=SCB[2][0:3], zb=2, tb=4, qr=qr4[1], QB=QR4[1], c0=48, lock="zqB")]
                ISET = [dict(z=sc[1][:, 0:512], a1=sc[1][:, 512:1024], a2=sc[1][:, 1024:1536], ZB=SCB[1][0:3], zb=1, tb=7, qr=qr4[2], QB=QR4[2], lock="ziA"),
                        dict(z=f32v(36864), a1=f32v(38912), a2=f32v(40960), ZB=[RB[0], RB[1], MKB], zb=3, tb=5, qr=qr4[3], QB=QR4[3], lock="ziB")]

                def rope8(t, z, a1, a2, dst, ZB, DB):
                    z4 = z.rearrange("p (h k i) -> p h k i", h=8, k=2)
                    zz = z.rearrange("p (h d) -> p h d", h=8)
                    a13 = a1.rearrange("p (h k i) -> p h k i", h=8, k=2)
                    a23 = a2.rearrange("p (h d) -> p h d", h=8)
                    cb = cosT[:, t, :].unsqueeze(1).unsqueeze(1).to_broadcast([128, 8, 2, 32])
                    S.add("dve", TT(a13, z4, cb, ALU.mult), reads=[ZB[0], TAB], writes=[ZB[1]], dur=0.65)
                    S.add("dve", TT(a23[:, :, 0:32], zz[:, :, 32:64], nsinT[:, t, :].unsqueeze(1).to_broadcast([128, 8, 32]), ALU.mult),
                          reads=[ZB[0], TAB], writes=[ZB[2]], dur=0.4)
                    S.add("dve", TT(a23[:, :, 32:64], zz[:, :, 0:32], sinT[:, t, :].unsqueeze(1).to_broadcast([128, 8, 32]), ALU.mult),
                          reads=[ZB[0], TAB], writes=[ZB[2]], dur=0.4)
                    yield None
                    S.add("dve", TT(dst, a1, a2, ALU.add), reads=[ZB[1], ZB[2], MSB[1]], writes=[DB], dur=0.65)

                def q_thread(t, tl, st):
                    yield ("acquire", st["lock"])
                    zq, t1, t2, ZQB = st["z"], st["a1"], st["a2"], st["ZB"]
                    for c in range(8):
                        S.add("pe", MM(bank(st["zb"]), hTc[:, c, tl * 128:(tl + 1) * 128], wq[:, c, :], c == 0, c == 7), reads=[HCT[tl], wqb], writes=[PB[st["zb"]]])
                    S.add("act", ACT(zq, bank(st["zb"]), AF.Copy), reads=[PB[st["zb"]]], writes=[ZQB[0]], dur=0.75)
                    yield None
                    S.add("dve", TT(t1, zq, zq, ALU.mult), reads=[ZQB[0]], writes=[ZQB[1]], dur=0.65)
                    ssq = sm[:, st["c0"]:st["c0"] + 8]
                    S.add("dve", lambda e, o=ssq, i=t1.rearrange("p (h d) -> p h d", h=8): e.tensor_reduce(out=o, in_=i, axis=AX.X, op=ALU.add),
                          reads=[ZQB[1]], writes=[st["QB"]], dur=0.7)
                    S.add("dve", TS(ssq, ssq, 1.0 / 64, EPS, ALU.mult, ALU.add), reads=[st["QB"]], writes=[st["QB"]], dur=0.2)
                    rq = sm[:, st["c0"] + 8:st["c0"] + 16]
                    S.add("pool", TT(rq, ssq, mhalf[:, 0:8], ALU.pow), reads=[st["QB"], CST], writes=[st["QB"]], dur=1.4)
                    yield None
                    z3 = zq.rearrange("p (h d) -> p h d", h=8)
                    S.add("dve", TT(z3, z3, rq.unsqueeze(2).to_broadcast([128, 8, 64]), ALU.mult), reads=[ZQB[0], st["QB"]], writes=[ZQB[0]], dur=0.65)
                    S.add("dve", TT(z3, z3, gq.unsqueeze(1).to_broadcast([128, 8, 64]), ALU.mult), reads=[ZQB[0], CST], writes=[ZQB[0]], dur=0.65)
                    yield None
                    yield from rope8(t, zq, t1, t2, st["qr"], ZQB, st["QB"])
                    tq = bank_bf(st["tb"])
                    for j in range(4):
                        S.add("pe", TR(tq[:, j * 128:(j + 1) * 128], st["qr"][:, j * 128:(j + 1) * 128], ident), reads=[st["QB"], IDB], writes=[PB[st["tb"]]], dur=0.3)
                    S.add("act", ACT(qT[:, :, tl * 128:(tl + 1) * 128], tq[:, 0:512].rearrange("p (j t) -> p j t", j=4), AF.Copy), reads=[PB[st["tb"]]], writes=[QTB], dur=0.75)
                    yield ("release", st["lock"])

                def qi_thread(t, tl, st):
                    yield ("acquire", st["lock"])
                    zqi, u1, u2, ZIB = st["z"], st["a1"], st["a2"], st["ZB"]
                    for c in range(8):
                        S.add("pe", MM(bank(st["zb"]), hTc[:, c, tl * 128:(tl + 1) * 128], wqi[:, c, :], c == 0, c == 7), reads=[HCT[tl], wqib], writes=[PB[st["zb"]]])
                    S.add("act", ACT(zqi, bank(st["zb"]), AF.Copy), reads=[PB[st["zb"]]], writes=[ZIB[0]], dur=0.75)
                    yield None
                    yield from rope8(t, zqi, u1, u2, st["qr"], ZIB, st["QB"])
                    tqi = bank_bf(st["tb"])
                    for j in range(4):
                        S.add("pe", TR(tqi[:, j * 128:(j + 1) * 128], st["qr"][:, j * 128:(j + 1) * 128], ident), reads=[st["QB"], IDB], writes=[PB[st["tb"]]], dur=0.3)
                    S.add("act", ACT(qiT[:, :, tl * 128:(tl + 1) * 128], tqi[:, 0:512].rearrange("p (j t) -> p j t", j=4), AF.Copy), reads=[PB[st["tb"]]], writes=[QITB], dur=0.75)
                    yield ("release", st["lock"])

                def nrm_thread():
                    for tl in range(4):
                        yield ("acquire", "nrm%d" % tl)
                    for tl in range(4):
                        t = ch * 4 + tl
                        norm_tile(t, hn2[t % 2], HN2[t % 2], 4 + (t % 2), 1, hTc[:, :, tl * 128:(tl + 1) * 128], [HCT[tl]] + ([HCB] if tl == 3 else []))
                        yield ("release", "nrm%d" % tl)

                def gated(gen, tl):
                    yield ("acquire", "nrm%d" % tl)
                    yield ("release", "nrm%d" % tl)
                    yield from gen

                ths = [nrm_thread()]
                for tl in range(4):
                    ths.append(gated(q_thread(ch * 4 + tl, tl, QSET[tl % 2]), tl))
                    ths.append(gated(qi_thread(ch * 4 + tl, tl, ISET[tl % 2]), tl))
                run_threads(ths, max_active=5)

                if ch == 1:
                    dbg("qT", qT, [QTB])
                    dbg("qiT", qiT, [QITB])
                half0 = load_half(0)
                run_threads([tile_thread(ch, ch * 4 + tl, (ch * 4 + tl) % NTH, use_act=(tl != 2 or ch == 0)) for tl in (3, 2, 1, 0)], max_active=NTH, offsets=None)

                if ch == 1:
                    dbg("yaT", yaT, [YAB])
                mg = sc[0].bitcast(BF16).rearrange("p (c t) -> p c t", c=8)[:, :, 0:512] if False else view(Q0 + 12288, 8192, BF16).rearrange("p (c t) -> p c t", c=8)
                sg = [sc[1][:, 0:512], sc[1][:, 512:1024]]
                m1 = sc[1][:, 1024:1536]
                m2 = sc[1][:, 1536:2048]
                MGB = SCB[0]
                for half in range(2):
                    wga, wgab, wgp, wgpb, wpa, wpp, wpab = half0 if half == 0 else load_half(1)
                    wppb = wpab
                    for dl in range(4):
                        dmc = half * 4 + dl
                        for c in range(8):
                            S.add("pe", MM(bank(0), wga[:, c, dl * 128:(dl + 1) * 128], hTc[:, c, :], c == 0, c == 7), reads=[wgab, HCB] + HCT, writes=[PB[0]])
                        S.add("act", ACT(sg[0], bank(0), AF.Sigmoid), reads=[PB[0]], writes=[SCB[1][0]])
                        for j in range(4):
                            S.add("pe", MM(bank(1), wpa[:, j, dl * 128:(dl + 1) * 128], yaT[:, j, :], j == 0, j == 3), reads=[wpab, YAB], writes=[PB[1]])
                        S.add("dve", TT(m1, sg[0], bank(1), ALU.mult), reads=[SCB[1][0], PB[1]], writes=[SCB[1][2]])
                        for c in range(8):
                            S.add("pe", MM(bank(2), wgp[:, c, dl * 128:(dl + 1) * 128], hTc[:, c, :], c == 0, c == 7), reads=[wgpb, HCB] + HCT, writes=[PB[2]])
                        S.add("act", ACT(sg[1], bank(2), AF.Sigmoid), reads=[PB[2]], writes=[SCB[1][1]])
                        for g in range(4):
                            S.add("pe", MM(bank(3), wpp[:, g, dl * 128:(dl + 1) * 128], ypT[:, g, ch * 512:(ch + 1) * 512], g == 0, g == 3),
                                  reads=[wppb, YPB[ch]], writes=[PB[3]])
                        S.add("dve", TT(m2, sg[1], bank(3), ALU.mult), reads=[SCB[1][1], PB[3]], writes=[SCB[1][3]])
                        S.add("dve", TT(mg[:, dmc, :], m1, m2, ALU.add), reads=[SCB[1][2], SCB[1][3]], writes=[MGB[dmc // 2]])
                if ch == 1:
                    dbg("mg", mg, MGB)
                wos = [load_w(lambda s: s.rearrange("p (c d) -> p c d", c=8), wout_d[:, dh * 512:(dh + 1) * 512].rearrange("(c p) d -> p c d", p=128)) for dh in range(2)]
                if ch < 3:
                    qw_next = load_q()
                else:
                    vf0 = lambda s: s[:, 0:8 * 512].rearrange("p (c f) -> p c f", c=8)
                    a_ = load_w(vf0, w1_d[1][:, 0:512].rearrange("(c p) f -> p c f", p=128))
                    b_ = load_w(vf0, w3_d[1][:, 0:512].rearrange("(c p) f -> p c f", p=128))
                    preload[1] = a_ + b_
                for dh in range(2):
                    wo, wob = wos[dh]
                    for tl in range(4):
                        t = ch * 4 + tl
                        b = 4 + (tl % 2)
                        for c in range(8):
                            S.add("pe", MM(bank(b), mg[:, c, tl * 128:(tl + 1) * 128], wo[:, c, :], c == 0, c == 7), reads=MGB + [wob], writes=[PB[b]])
                        xv = xs[:, t, dh * 512:(dh + 1) * 512]
                        S.add("dve", TT(xv, xv, bank(b), ALU.add), reads=[PB[b], XB[t]], writes=[XB[t]])

        stage = debug.get("_stage", 3) if isinstance(debug.get("_stage", 3), int) else 3
        ffn(0, final=False)
        dbg("x1", xs, XB)
        mix()
        dbg("x2", xs, XB)
        stores = ffn(1, final=True)
        S.emit(final_wait=stores)
    return nc


_NC_CACHE = {}


def _prep_consts(inp):
    cst = np.zeros((128, 256), np.float32)
    for k, nm in enumerate(["ffn1_norm", "mix_norm", "ffn2_norm"]):
        cst[:, k * 8:(k + 1) * 8] = np.asarray(inp[nm], np.float32).reshape(8, 128).T
    cst[:, 24:28] = np.asarray(inp["pool_b"], np.float32).reshape(4, 128).T
    cst[:, 28:32] = np.asarray(inp["pool_scale"], np.float32).reshape(4, 128).T
    cst[:, 32:96] = np.asarray(inp["q_norm"], np.float32).reshape(1, 64)
    cst[:, 96:160] = np.asarray(inp["k_norm"], np.float32).reshape(1, 64)
    cst[:, 160:192] = (10000.0 ** (-np.arange(0, 64, 2, dtype=np.float32) / 64)).astype(np.float32)[None, :]
    cst[:, 192:208] = (1.0 / np.arange(1, 17, dtype=np.float32))[None, :]
    cst[:, 208:224] = -0.5
    cst[:, 224:256] = (0.5 ** np.arange(1, 33, dtype=np.float64)).astype(np.float32)[None, :]
    return cst


def kernel(**inputs):
    inp = {k: np.asarray(v) for k, v in inputs.items()}
    if "nc" not in _NC_CACHE:
        _NC_CACHE["nc"] = build_nc()
    nc = _NC_CACHE["nc"]
    cst = _prep_consts(inp)
    shared = {
        "cst": cst,
        "ffn1_w1": np.ascontiguousarray(inp["ffn1_w1"][0], np.float32),
        "ffn1_w3": np.ascontiguousarray(inp["ffn1_w3"][0], np.float32),
        "ffn1_w2": np.ascontiguousarray(inp["ffn1_w2"][0], np.float32),
        "ffn2_w1": np.ascontiguousarray(inp["ffn2_w1"][0], np.float32),
        "ffn2_w3": np.ascontiguousarray(inp["ffn2_w3"][0], np.float32),
        "ffn2_w2": np.ascontiguousarray(inp["ffn2_w2"][0], np.float32),
        "w_in": np.ascontiguousarray(inp["w_in"][0], np.float32),
        "pool_w": np.ascontiguousarray(inp["pool_w"][0], np.float32),
        "proj_attn": np.ascontiguousarray(inp["proj_attn"][0], np.float32),
        "proj_pool": np.ascontiguousarray(inp["proj_pool"][0], np.float32),
        "w_out": np.ascontiguousarray(inp["w_out"][0], np.float32),
    }
    x = np.asarray(inp["x"], np.float32)
    pos = np.asarray(inp["positions"], np.int32)
    in_maps = []
    for b in range(8):
        m = dict(shared)
        m["x"] = np.ascontiguousarray(x[b])
        m["pos"] = np.ascontiguousarray(pos[b].reshape(NT, 128).T)
        in_maps.append(m)
    res = run_bass_kernel_spmd(nc, in_maps, core_ids=list(range(8)))
    return np.stack([np.asarray(r["out"], np.float32) for r in res.results], axis=0)
```
